# Optimizing a Trainium2 kernel written in Bass

```python
import jax, jax.numpy as jnp
from jax import lax
import numpy as np

D_MODEL = 1024
BATCH = 8
SEQ = 4096
DEPTH = 2

GRID_W = 64
CTX_LEN = 256
HG_DIM = 128
HG_WIDTH = D_MODEL // 2
HG_HEADS = HG_WIDTH // HG_DIM
MLP_WIDTH = D_MODEL - HG_WIDTH
MLP_HEADS = 4
MLP_DIM = MLP_WIDTH // MLP_HEADS
MLP_CHUNK = 128
SCAN_CHUNK = 32
MIX_WIDTH = HG_WIDTH + MLP_WIDTH
PROJ_WIDTH = 5 * HG_WIDTH + 2 * MLP_WIDTH
D_FF = 2816
CONV_W = 3
EPS = 1e-6

kernel_name = "hybrid_hgrn2_gmlp_dit_prefix"


def rmsnorm(x, gain):
    xf = x.astype(jnp.float32)
    y = xf * lax.rsqrt(jnp.mean(xf * xf, axis=-1, keepdims=True) + EPS)
    return (y * gain.astype(jnp.float32)).astype(x.dtype)


def layernorm(x, gain, bias):
    xf = x.astype(jnp.float32)
    mu = jnp.mean(xf, axis=-1, keepdims=True)
    var = jnp.mean(jnp.square(xf - mu), axis=-1, keepdims=True)
    y = (xf - mu) * lax.rsqrt(var + EPS) * gain.astype(jnp.float32) + bias.astype(jnp.float32)
    return y.astype(x.dtype)


def modulate(h, shift, scale):
    return h * (1.0 + scale) + shift


def split_heads(a, n_heads):
    b, t, w = a.shape
    return a.reshape(b, t, n_heads, w // n_heads).transpose(0, 2, 1, 3)


def merge_heads(a):
    b, h, t, d = a.shape
    return a.transpose(0, 2, 1, 3).reshape(b, t, h * d)


def flip_t(a):
    return jnp.flip(a, axis=2)


def lower_bounds(logits):
    cum = jnp.cumsum(jax.nn.softmax(logits.astype(jnp.float32), axis=0), axis=0)
    return cum - cum[0:1]


def forget_gate(f_logit, lb, first):
    f = f_logit.astype(jnp.float32)
    if first:
        return jax.nn.log_sigmoid(f), jax.nn.sigmoid(-f)
    gate = lb + (1.0 - lb) * jax.nn.sigmoid(f)
    return jnp.log(gate), 1.0 - gate


def hgrn2_gates(p, lb_f, lb_b, first):
    w = HG_WIDTH
    logf_f, k_f = forget_gate(p[..., :w], lb_f, first)
    logf_b, k_b = forget_gate(p[..., w:2 * w], lb_b, first)
    v = p[..., 2 * w:3 * w].astype(jnp.float32)
    return tuple(split_heads(a, HG_HEADS) for a in (logf_f, k_f, logf_b, k_b, v))


def advance_state(h, k, v, b):
    b_last = b[:, :, -1:, :]
    return (jnp.exp(b_last)[:, :, 0, :, None] * h
            + jnp.einsum('bhsk,bhsv->bhkv', k * jnp.exp(b_last - b), v))


def chunk_gla(q, k, v, log_f, h0):
    b_, h_, t, _ = q.shape
    n = t // SCAN_CHUNK

    def to_chunks(a):
        return jnp.moveaxis(a.reshape(b_, h_, n, SCAN_CHUNK, a.shape[-1]), 2, 0)

    mask = jnp.tril(jnp.ones((SCAN_CHUNK, SCAN_CHUNK), dtype=bool))[:, :, None]

    def step(h, blk):
        qc, kc, vc, gc = blk
        b = jnp.cumsum(gc, axis=2)
        o_inter = jnp.einsum('bhtk,bhkv->bhtv', qc * jnp.exp(b), h)
        diff = b[:, :, :, None, :] - b[:, :, None, :, :]
        decay = jnp.where(mask, jnp.exp(jnp.where(mask, diff, 0.0)), 0.0)
        scores = jnp.einsum('bhtk,bhsk,bhtsk->bhts', qc, kc, decay)
        o = o_inter + jnp.einsum('bhts,bhsv->bhtv', scores, vc)
        return advance_state(h, kc, vc, b), o

    h_end, o = lax.scan(step, h0, (to_chunks(q), to_chunks(k), to_chunks(v), to_chunks(log_f)))
    o = jnp.moveaxis(o, 0, 2).reshape(b_, h_, t, v.shape[-1])
    return o, h_end


def chunk_sgu(u, v, ln_g, ln_b, w_s, b_s):
    bsz, t, _ = u.shape
    n = t // MLP_CHUNK
    u = jax.nn.gelu(u, approximate=False).reshape(bsz, n, MLP_CHUNK, MLP_HEADS, MLP_DIM)
    v = jax.nn.gelu(v, approximate=False).reshape(bsz, n, MLP_CHUNK, MLP_HEADS, MLP_DIM)
    v = layernorm(v, ln_g.reshape(MLP_HEADS, MLP_DIM), ln_b.reshape(MLP_HEADS, MLP_DIM))
    z = jnp.einsum('hpq,bnqhc->bnphc', w_s.astype(v.dtype), v) + jnp.transpose(b_s)[None, None, :, :, None]
    return (u * z).reshape(bsz, t, MLP_WIDTH)


def token_mixers(p, lb_f, lb_b, first, h0_f, h0_b, hg_gain, sgu_g, sgu_b, w_s, b_s):
    w = HG_WIDTH
    logf_f, k_f, logf_b, k_b, v = hgrn2_gates(p, lb_f, lb_b, first)
    q = split_heads(jax.nn.silu(p[..., 3 * w:4 * w].astype(jnp.float32)), HG_HEADS) * (HG_DIM ** -0.5)
    o_f, h_f = chunk_gla(q, k_f, v, logf_f, h0_f)
    o_b, h_b = chunk_gla(flip_t(q), flip_t(k_b), flip_t(v), flip_t(logf_b), h0_b)
    o = rmsnorm(o_f + flip_t(o_b), hg_gain.reshape(HG_HEADS, 1, HG_DIM))
    o_hg = merge_heads(o).astype(p.dtype) * jax.nn.silu(p[..., 4 * w:5 * w])
    o_mlp = chunk_sgu(p[..., 5 * w:5 * w + MLP_WIDTH], p[..., 5 * w + MLP_WIDTH:], sgu_g, sgu_b, w_s, b_s)
    return jnp.concatenate([o_hg, o_mlp], axis=-1), h_f, h_b


def conv_ffn(h, w_up, taps, conv_b, w_down, rows):
    a, g = jnp.split(h @ w_up, 2, axis=-1)
    bsz, t, f = g.shape
    if rows is None:
        g2, tp = g.reshape(bsz, 1, t, f), taps[1:2]
    else:
        g2, tp = g.reshape(bsz, rows, GRID_W, f), taps
    gc = lax.conv_general_dilated(g2, tp[:, :, None, :].astype(g.dtype), (1, 1), 'SAME',
                                  dimension_numbers=('NHWC', 'HWIO', 'NHWC'), feature_group_count=f)
    gc = gc.reshape(bsz, t, f) + conv_b
    return (a * jax.nn.gelu(gc, approximate=False)) @ w_down


def setup_inputs(seed: int = 0) -> dict:
    key = jax.random.key(seed)
    ks = jax.random.split(key, 22)
    nrm = jax.random.normal
    L, D, F = DEPTH, D_MODEL, D_FF
    return {
        "x": nrm(ks[0], (BATCH, SEQ, D), jnp.float32),
        "c": nrm(ks[1], (BATCH, D), jnp.float32),
        "ctx": nrm(ks[2], (BATCH, CTX_LEN, D), jnp.float32),
        "c_ctx": nrm(ks[3], (D,), jnp.float32),
        "w_ada": nrm(ks[4], (L, D, 6 * D), jnp.float32) * (0.5 * D ** -0.5),
        "b_ada": nrm(ks[5], (L, 6 * D), jnp.float32) * 0.02,
        "norm_mix": 1.0 + 0.1 * nrm(ks[6], (L, D), jnp.float32),
        "norm_ffn": 1.0 + 0.1 * nrm(ks[7], (L, D), jnp.float32),
        "w_in": nrm(ks[8], (L, D, PROJ_WIDTH), jnp.float32) * D ** -0.5,
        "lb_logits_fwd": nrm(ks[9], (L, HG_WIDTH), jnp.float32),
        "lb_logits_bwd": nrm(ks[10], (L, HG_WIDTH), jnp.float32),
        "hg_norm": 1.0 + 0.1 * nrm(ks[11], (L, HG_WIDTH), jnp.float32),
        "sgu_norm_g": 1.0 + 0.1 * nrm(ks[12], (L, MLP_WIDTH), jnp.float32),
        "sgu_norm_b": 0.02 * nrm(ks[13], (L, MLP_WIDTH), jnp.float32),
        "w_spatial": nrm(ks[14], (L, MLP_HEADS, MLP_CHUNK, MLP_CHUNK), jnp.float32) * MLP_CHUNK ** -0.5,
        "b_spatial": 1.0 + 0.1 * nrm(ks[15], (L, MLP_HEADS, MLP_CHUNK), jnp.float32),
        "w_out": nrm(ks[16], (L, MIX_WIDTH, D), jnp.float32) * MIX_WIDTH ** -0.5,
        "w_up": nrm(ks[17], (L, D, 2 * F), jnp.float32) * D ** -0.5,
        "conv_w": nrm(ks[18], (L, CONV_W, CONV_W, F), jnp.float32) / CONV_W,
        "conv_b": 0.02 * nrm(ks[19], (L, F), jnp.float32),
        "w_down": nrm(ks[20], (L, F, D), jnp.float32) * F ** -0.5,
        "norm_final": 1.0 + 0.1 * nrm(ks[21], (D,), jnp.float32),
    }


def reference(x, c, ctx, c_ctx, w_ada, b_ada, norm_mix, norm_ffn, w_in, lb_logits_fwd, lb_logits_bwd,
              hg_norm, sgu_norm_g, sgu_norm_b, w_spatial, b_spatial, w_out, w_up, conv_w, conv_b, w_down,
              norm_final):
    w = HG_WIDTH
    rows = x.shape[1] // GRID_W
    bsz = x.shape[0]
    lb_f_all = lower_bounds(lb_logits_fwd)
    lb_b_all = lower_bounds(lb_logits_bwd)
    ada = jnp.einsum('bd,lde->lbe', jax.nn.silu(c), w_ada) + b_ada[:, None, :]
    ada_c = jnp.einsum('d,lde->le', jax.nn.silu(c_ctx), w_ada) + b_ada
    h0 = jnp.zeros((bsz, HG_HEADS, HG_DIM, HG_DIM), jnp.float32)
    xc = ctx
    for l in range(DEPTH):
        first, last = l == 0, l == DEPTH - 1
        sh1, sc1, g1, sh2, sc2, g2 = (m[:, None, :] for m in jnp.split(ada[l], 6, axis=-1))
        csh1, csc1, cg1, csh2, csc2, cg2 = jnp.split(ada_c[l], 6, axis=-1)
        lb_f, lb_b = lb_f_all[l], lb_b_all[l]
        mix_args = (hg_norm[l], sgu_norm_g[l], sgu_norm_b[l], w_spatial[l], b_spatial[l])

        hc = modulate(rmsnorm(xc, norm_mix[l]), csh1, csc1)
        if last:
            logf_f, k_f, logf_b, k_b, vc = hgrn2_gates(hc @ w_in[l][:, :3 * w], lb_f, lb_b, first)
            hf_c = advance_state(h0, k_f, vc, jnp.cumsum(logf_f, axis=2))
            hb_c = advance_state(h0, flip_t(k_b), flip_t(vc), jnp.cumsum(flip_t(logf_b), axis=2))
        else:
            oc, hf_c, hb_c = token_mixers(hc @ w_in[l], lb_f, lb_b, first, h0, h0, *mix_args)

        h = modulate(rmsnorm(x, norm_mix[l]), sh1, sc1)
        o, _, _ = token_mixers(h @ w_in[l], lb_f, lb_b, first, hf_c, hb_c, *mix_args)
        x = x + g1 * (o @ w_out[l])
        h2 = modulate(rmsnorm(x, norm_ffn[l]), sh2, sc2)
        x = x + g2 * conv_ffn(h2, w_up[l], conv_w[l], conv_b[l], w_down[l], rows)

        if not last:
            xc = xc + cg1 * (oc @ w_out[l])
            hc2 = modulate(rmsnorm(xc, norm_ffn[l]), csh2, csc2)
            xc = xc + cg2 * conv_ffn(hc2, w_up[l], conv_w[l], conv_b[l], w_down[l], None)
    return rmsnorm(x, norm_final)
```

```python
import numpy as np
from collections import deque
from contextlib import ExitStack

import concourse.bass as bass
import concourse.mybir as mybir
from concourse.bass_utils import run_bass_kernel_spmd

F32 = mybir.dt.float32
BF16 = mybir.dt.bfloat16
AF = mybir.ActivationFunctionType
ALU = mybir.AluOpType
AX = mybir.AxisListType

COMPUTE = ("pe", "act", "dve", "pool")
ENGS = ("pe", "act", "dve", "pool", "sp")

D = 1024
T_LAT = 4096
T_CTX = 256
NT_LAT = 32
NT_CTX = 2
NT = 34
HW = 512
PROJ = 3584
DFF = 2816
NFC = 22
EPS = 1e-6
C_FF, C_FB, C_I, C_Q, C_G, C_U, C_V = 0, 512, 1024, 1536, 2048, 2560, 3072


class Tile:
    __slots__ = ("name", "writers", "readers")

    def __init__(self, name, init=None):
        self.name = name
        self.writers = list(init) if init else []
        self.readers = []


class Op:
    __slots__ = ("eng", "fn", "deps", "inc", "val", "dma", "sem", "prev_dma")

    def __init__(self, eng, fn, dma):
        self.eng = eng
        self.fn = fn
        self.deps = []
        self.inc = False
        self.val = 0
        self.dma = dma
        self.sem = None
        self.prev_dma = None


class Prog:
    def __init__(self, nc, n_dma_sems=16):
        self.nc = nc
        self.ops = []
        self.n_dma_sems = n_dma_sems
        self.last_op = {}
        self.recent_dma = {e: deque(maxlen=n_dma_sems) for e in ENGS}

    def frontier(self):
        f = [o for o in self.last_op.values()]
        for e in ENGS:
            f.extend(self.recent_dma[e])
        return f

    def op(self, eng, fn, reads=(), writes=(), dma=False):
        o = Op(eng, fn, dma)
        deps = {}
        for t in reads:
            for w in t.writers:
                if (not dma) and (not w.dma) and w.eng == eng and eng == "pe":
                    continue
                deps[id(w)] = w
        for t in writes:
            for w in t.writers:
                if (not dma) and (not w.dma) and w.eng == eng:
                    continue
                deps[id(w)] = w
            for r in t.readers:
                if (not dma) and (not r.dma) and r.eng == eng:
                    continue
                deps[id(r)] = r
        for t in reads:
            t.readers.append(o)
        for t in writes:
            t.writers = [o]
            t.readers = []
        o.deps = list(deps.values())
        for d in o.deps:
            d.inc = True
        self.ops.append(o)
        if dma:
            self.recent_dma[eng].append(o)
        else:
            self.last_op[eng] = o
        return o

    def emit(self):
        nc = self.nc
        by_eng = {e: [] for e in ENGS}
        for o in self.ops:
            by_eng[o.eng].append(o)
        stats = {}
        with ExitStack() as es:
            esem = {e: es.enter_context(nc.semaphore("s_" + e)) for e in COMPUTE}
            dsem = {}
            for e in ENGS:
                if any(o.dma for o in by_eng[e]):
                    dsem[e] = [es.enter_context(nc.semaphore("d_%s_%d" % (e, i)))
                               for i in range(self.n_dma_sems)]
            for e in ENGS:
                cnt = 0
                dcnt = [0] * self.n_dma_sems
                last_on_slot = [None] * self.n_dma_sems
                k = 0
                for o in by_eng[e]:
                    if o.dma:
                        j = k % self.n_dma_sems
                        k += 1
                        dcnt[j] += 16
                        o.sem = dsem[e][j]
                        o.val = dcnt[j]
                        o.prev_dma = last_on_slot[j]
                        last_on_slot[j] = o
                    else:
                        if o.inc:
                            cnt += 1
                        o.sem = esem[e]
                        o.val = cnt
            block = es.enter_context(nc.Block())

            def run_engine(e, engobj):
                known = {}
                nwait = [0]

                def wait(sem, val):
                    if val <= 0:
                        return
                    key = sem.num
                    if known.get(key, 0) >= val:
                        return
                    engobj.wait_ge(sem, val)
                    known[key] = val
                    nwait[0] += 1

                for o in by_eng[e]:
                    for d in o.deps:
                        wait(d.sem, d.val)
                    if o.dma and o.prev_dma is not None:
                        wait(o.prev_dma.sem, o.prev_dma.val)
                    ins = o.fn(engobj)
                    if o.dma:
                        ins.then_inc(o.sem, 16)
                    elif o.inc:
                        ins.then_inc(o.sem, 1)
                if e in dsem:
                    last = {}
                    for o in by_eng[e]:
                        if o.dma:
                            last[o.sem.num] = o
                    for o in last.values():
                        wait(o.sem, o.val)
                stats[e] = (len(by_eng[e]), nwait[0])

            @block.sync
            def _(eng):
                run_engine("sp", eng)

            @block.tensor
            def _(eng):
                run_engine("pe", eng)

            @block.scalar
            def _(eng):
                run_engine("act", eng)

            @block.vector
            def _(eng):
                run_engine("dve", eng)

            @block.gpsimd
            def _(eng):
                run_engine("pool", eng)
        return stats


class Buf:
    __slots__ = ("ap", "t")

    def __init__(self, ap, t):
        self.ap = ap
        self.t = t

    def __getitem__(self, idx):
        return self.ap[idx]


class Ring:
    def __init__(self, bufs):
        self.bufs = bufs
        self.i = 0

    def get(self):
        b = self.bufs[self.i % len(self.bufs)]
        self.i += 1
        return b


class Arena:
    def __init__(self, P, arena_ap, nbytes):
        self.P = P
        self.a = arena_ap
        self.n = nbytes
        self.off = 0
        self.reused = False
        self.peak = 0

    def mark(self):
        return self.off

    def release(self, m):
        self.off = m
        self.reused = True

    def alloc(self, name, free_shape, dtype, parts=128):
        esz = 4 if dtype == F32 else 2
        n = 1
        for s in free_shape:
            n *= s
        nb = (n * esz + 63) // 64 * 64
        assert self.off + nb <= self.n, "arena overflow at %s: %d + %d > %d" % (name, self.off, nb, self.n)
        ap = self.a[0:parts, self.off // 2:self.off // 2 + (n * esz) // 2]
        if dtype == F32:
            ap = ap.bitcast(F32)
        if len(free_shape) == 2:
            ap = ap.rearrange("p (a b) -> p a b", b=free_shape[1])
        elif len(free_shape) == 3:
            ap = ap.rearrange("p (a b c) -> p a b c", b=free_shape[1], c=free_shape[2])
        self.off += nb
        self.peak = max(self.peak, self.off)
        t = Tile(name, self.P.frontier() if self.reused else None)
        return Buf(ap, t)

    def ring(self, name, n, free_shape, dtype):
        return Ring([self.alloc("%s%d" % (name, i), free_shape, dtype) for i in range(n)])


def T(*bufs):
    return [b.t for b in bufs]


def host_consts():
    c = np.zeros((128, 1024), np.float32)
    c[:, 0:128] = np.eye(128, dtype=np.float32)
    s = np.arange(128)[:, None]
    t = np.arange(128)[None, :]
    same = (s // 32) == (t // 32)
    c[:, 128:256] = (s <= t).astype(np.float32)
    c[:, 256:384] = (s >= t).astype(np.float32)
    j = np.arange(512)
    c[:, 384:896] = (j % 32 != 0).astype(np.float32)[None, :]
    for cc in range(4):
        c[:, 896 + cc] = ((np.arange(128) // 32) == cc).astype(np.float32)
    c[:, 900] = 1.0
    return c


def build_program(n_layers=2, debug=False):
    nc = bass.Bass("TRN2", target_bir_lowering=False)
    es = ExitStack()

    def din(name, shape):
        return nc.dram_tensor(name, list(shape), F32, kind="ExternalInput").ap()

    x_d = din("x", [T_LAT, D])
    ctx_d = din("ctx", [T_CTX, D])
    cc_d = din("cc", [128, 8, 2])
    consts_d = din("consts", [128, 1024])
    w_ada_d = din("w_ada", [2, D, 6 * D])
    b_ada_d = din("b_ada", [2, 6 * D])
    norm_mix_d = din("norm_mix", [2, D])
    norm_ffn_d = din("norm_ffn", [2, D])
    w_in_d = din("w_in", [2, D, PROJ])
    lbf_d = din("lb_logits_fwd", [2, HW])
    lbb_d = din("lb_logits_bwd", [2, HW])
    hg_norm_d = din("hg_norm", [2, HW])
    sgu_g_d = din("sgu_norm_g", [2, HW])
    sgu_b_d = din("sgu_norm_b", [2, HW])
    w_sp_d = din("w_spatial", [2, 4, 128, 128])
    b_sp_d = din("b_spatial", [2, 4, 128])
    w_out_d = din("w_out", [2, D, D])
    w_up_d = din("w_up", [2, D, 2 * DFF])
    conv_w_d = din("conv_w", [2, 3, 3, DFF])
    conv_b_d = din("conv_b", [2, DFF])
    w_down_d = din("w_down", [2, DFF, D])
    norm_final_d = din("norm_final", [D])
    out_d = nc.dram_tensor("out", [T_LAT, D], F32, kind="ExternalOutput").ap()

    if debug:
        kw = dict(kind="ExternalOutput")
        xA_ds = [nc.dram_tensor("xA%d" % l, [NT * 128, D], F32, **kw).ap() for l in range(2)]
        xB_ds = [nc.dram_tensor("xB%d" % l, [NT * 128, D], F32, **kw).ap() for l in range(2)]
        ob_ds = [nc.dram_tensor("obs%d" % l, [NT * 128, HW], F32, **kw).ap() for l in range(2)]
        ada_d = nc.dram_tensor("ada_s", [2, 2, 6 * D], F32, **kw).ap()
    else:
        xA_ds = [nc.dram_tensor("xA", [NT * 128, D], F32).ap()] * 2
        xB_ds = [nc.dram_tensor("xB", [NT * 128, D], F32).ap()] * 2
        ob_ds = [nc.dram_tensor("obs", [NT * 128, HW], F32).ap()] * 2
        ada_d = nc.dram_tensor("ada_s", [2, 2, 6 * D], F32).ap()

    wbf = {}
    for l_ in range(2):
        if l_ > 0:
            wbf[("w_in", l_)] = nc.dram_tensor("w_in_bf%d" % l_, [D, PROJ], BF16).ap()
            wbf[("w_out", l_)] = nc.dram_tensor("w_out_bf%d" % l_, [D, D], BF16).ap()
        wbf[("w_up", l_)] = nc.dram_tensor("w_up_bf%d" % l_, [D, 2 * DFF], BF16).ap()
        wbf[("w_dn", l_)] = nc.dram_tensor("w_dn_bf%d" % l_, [DFF, D], BF16).ap()
    wbf_t = {k: Tile("wbf_%s%d" % k) for k in wbf}

    P = Prog(nc)
    ARENA_BYTES = 207 * 1024
    arena_t = es.enter_context(nc.sbuf_tensor("arena", [128, ARENA_BYTES // 2], BF16))
    A = Arena(P, arena_t[:], ARENA_BYTES)
    ps_bufs = []
    for i in range(8):
        pt = es.enter_context(nc.psum_tensor("ps%d" % i, [128, 512], F32))
        ps_bufs.append(Buf(pt[:], Tile("ps%d" % i)))
    PS = Ring(ps_bufs)

    if debug:
        xA_ts = [[Tile("xA%d" % i) for i in range(NT)] for l in range(2)]
        xB_ts = [[Tile("xB%d" % i) for i in range(NT)] for l in range(2)]
        ob_ts = [[Tile("ob%d" % i) for i in range(NT)] for l in range(2)]
    else:
        xA_ts = [[Tile("xA%d" % i) for i in range(NT)]] * 2
        xB_ts = [[Tile("xB%d" % i) for i in range(NT)]] * 2
        ob_ts = [[Tile("ob%d" % i) for i in range(NT)]] * 2
    ada_t = [Tile("ada%d" % l) for l in range(2)]
    out_t = [Tile("out%d" % i) for i in range(NT_LAT)]

    def dma(q, out_ap, in_ap, reads, writes, **kw):
        return P.op(q, lambda e: e.dma_start(out=out_ap, in_=in_ap, **kw), reads, writes, dma=True)

    def mm(out_ap, lhsT, rhs, start, stop, reads, writes, skip=False):
        if skip:
            return P.op("pe", lambda e: e.matmul(out_ap, lhsT, rhs, start=start, stop=stop, skip_group_check=True),
                        reads, writes)
        return P.op("pe", lambda e: e.matmul(out_ap, lhsT, rhs, start=start, stop=stop), reads, writes)

    def act(out_ap, in_ap, func, reads, writes, eng="act", **kw):
        return P.op(eng, lambda e: e.activation(out=out_ap, in_=in_ap, func=func, **kw), reads, writes)

    def tt(eng, out_ap, a, b, op, reads, writes):
        return P.op(eng, lambda e: e.tensor_tensor(out_ap, a, b, op), reads, writes)

    def ts(eng, out_ap, a, s1, s2, op0, op1, reads, writes):
        if op1 is None:
            return P.op(eng, lambda e: e.tensor_scalar(out_ap, a, s1, None, op0), reads, writes)
        return P.op(eng, lambda e: e.tensor_scalar(out_ap, a, s1, s2, op0, op1), reads, writes)

    def stt(eng, out_ap, in0, scalar, in1, op0, op1, reads, writes):
        return P.op(eng, lambda e: e.scalar_tensor_tensor(out=out_ap, in0=in0, scalar=scalar, in1=in1,
                                                          op0=op0, op1=op1), reads, writes)

    def cp(eng, out_ap, in_ap, reads, writes):
        if eng == "act":
            return P.op("act", lambda e: e.copy(out_ap, in_ap), reads, writes)
        return P.op(eng, lambda e: e.tensor_copy(out_ap, in_ap), reads, writes)

    def recip(out_ap, in_ap, reads, writes):
        return P.op("dve", lambda e: e.reciprocal(out_ap, in_ap), reads, writes)

    def rsqrt_cols(col_out, col_in, scale, buf, tmpcol):
        ts("dve", tmpcol, col_in, scale, EPS, ALU.mult, ALU.add, T(buf), T(buf))
        act(tmpcol, tmpcol, AF.Ln, T(buf), T(buf))
        act(col_out, tmpcol, AF.Exp, T(buf), T(buf), scale=-0.5)

    cst = A.alloc("cst", [1024], F32)
    dma("sp", cst[:], consts_d[:, :], [], T(cst))
    ident_f = cst[:, 0:128]
    maskF = cst[:, 128:256]
    maskB = cst[:, 256:384]
    scanmask = cst[:, 384:896]
    rowmask = cst[:, 896:900]
    ident_b = A.alloc("ident_b", [128], BF16)
    cp("dve", ident_b[:], ident_f, T(cst), T(ident_b))

    lbc = A.alloc("lbc", [16], F32)
    scT = A.alloc("scT", [8, 2], F32)

    modc = A.alloc("modc", [2, 4, 8], F32)
    persist_mark = A.mark()

    def prologue():
        m0 = A.mark()
        lbl = A.alloc("lbl", [2, 2, 4], F32)
        for di, src in enumerate((lbf_d, lbb_d)):
            for l in range(2):
                dma("sp", lbl[:, di, l, :], src[l].rearrange("(h k) -> k h", k=128), [], T(lbl),
                    allow_slow_non_contiguous=True)
        for di in range(2):
            o0 = di * 8
            tt("dve", lbc[:, o0:o0 + 4], lbl[:, di, 1, :], lbl[:, di, 0, :], ALU.subtract, T(lbl), T(lbc))
            act(lbc[:, o0:o0 + 4], lbc[:, o0:o0 + 4], AF.Exp, T(lbc), T(lbc), scale=-1.0)
            ts("dve", lbc[:, o0:o0 + 4], lbc[:, o0:o0 + 4], 1.0, None, ALU.add, None, T(lbc), T(lbc))
            recip(lbc[:, o0:o0 + 4], lbc[:, o0:o0 + 4], T(lbc), T(lbc))
            ts("dve", lbc[:, o0 + 4:o0 + 8], lbc[:, o0:o0 + 4], -1.0, 1.0, ALU.mult, ALU.add, T(lbc), T(lbc))
        cct = A.alloc("cct", [8, 2], F32)
        tmp = A.alloc("cctmp", [8, 2], F32)
        dma("sp", cct[:], cc_d[:, :, :], [], T(cct))
        act(tmp[:], cct[:], AF.Exp, T(cct), T(tmp), scale=-1.0)
        ts("dve", tmp[:], tmp[:], 1.0, None, ALU.add, None, T(tmp), T(tmp))
        recip(tmp[:], tmp[:], T(tmp), T(tmp))
        tt("dve", scT[:], cct[:], tmp[:], ALU.mult, T(cct, tmp), T(scT))
        stage = A.ring("adastg", 2, [8, 512], F32)
        adarow = A.alloc("adarow", [6 * D], F32, parts=2)
        brow = A.alloc("brow", [6 * D], F32, parts=2)
        for l in range(n_layers):
            dma("sp", brow[:], b_ada_d[l:l + 1, :].broadcast_to([2, 6 * D]), [], T(brow))
            wv = w_ada_d[l].rearrange("(c p) n -> p c n", p=128)
            for cb in range(12):
                st = stage.get()
                dma("sp", st[:], wv[:, :, cb * 512:(cb + 1) * 512], [], T(st))
                ps = PS.get()
                for c in range(8):
                    mm(ps[0:2, :], scT[:, c, :], st[:, c, :], c == 0, c == 7, T(scT, st), T(ps))
                tt("dve", adarow[:, cb * 512:(cb + 1) * 512], ps[0:2, :], brow[:, cb * 512:(cb + 1) * 512],
                   ALU.add, T(ps, brow), T(adarow))
            dma("sp", ada_d[l], adarow[:], T(adarow), [ada_t[l]])
        A.release(m0)

    def layer_consts(l):
        m0 = A.mark()
        raw = A.alloc("modraw", [2, 4, 8], F32)
        nrm = A.alloc("nrm", [2, 8], F32)
        for m in range(2):
            for k, base in enumerate((0, 1024, 3072, 4096)):
                dma("sp", raw[:, m, k, :], ada_d[l, m, base:base + 1024].rearrange("(c p) -> p c", p=128),
                    [ada_t[l]], T(raw), allow_slow_non_contiguous=True)
        dma("sp", nrm[:, 0, :], norm_mix_d[l].rearrange("(c p) -> p c", p=128), [], T(nrm),
            allow_slow_non_contiguous=True)
        dma("sp", nrm[:, 1, :], norm_ffn_d[l].rearrange("(c p) -> p c", p=128), [], T(nrm),
            allow_slow_non_contiguous=True)
        for m in range(2):
            stt("dve", modc[:, m, 0, :], raw[:, m, 1, :], 1.0, nrm[:, 0, :], ALU.add, ALU.mult, T(raw, nrm), T(modc))
            cp("dve", modc[:, m, 1, :], raw[:, m, 0, :], T(raw), T(modc))
            stt("dve", modc[:, m, 2, :], raw[:, m, 3, :], 1.0, nrm[:, 1, :], ALU.add, ALU.mult, T(raw, nrm), T(modc))
            cp("dve", modc[:, m, 3, :], raw[:, m, 2, :], T(raw), T(modc))
        A.release(m0)

    def src_tile_ap(l, i, stage):
        if stage == "mix":
            if l == 0:
                if i < NT_CTX:
                    return ctx_d[i * 128:(i + 1) * 128, :], []
                return x_d[(i - 2) * 128:(i - 1) * 128, :], []
            return xB_ds[l - 1][i * 128:(i + 1) * 128, :], [xB_ts[l - 1][i]]
        return xA_ds[l][i * 128:(i + 1) * 128, :], [xA_ts[l][i]]

    def norm_transpose(xt, m, which, hT_out_fn, fr):
        st = fr["st"].get()
        xs = fr["xs"].get()
        junk = fr["junk"] if fr.get("junk") is not None else xs
        P.op("pool", lambda e: e.memset(st[:, 0:1], 0.0), [], T(st))
        act(junk[:], xt[:], AF.Square, T(xt), T(junk, st), accum_out=st[:, 0:1])
        rsqrt_cols(st[:, 2:3], st[:, 0:1], 1.0 / D, st, st[:, 1:2])
        act(xs[:], xt[:], AF.Identity, T(xt, st), T(xs), scale=st[:, 2:3])
        gi, si = (0, 1) if which == "mix" else (2, 3)
        for half in range(2):
            ps = fr.get("PS", PS).get()
            for cc in range(4):
                c = half * 4 + cc
                mm(ps[:, cc * 128:(cc + 1) * 128], xs[:, c * 128:(c + 1) * 128], ident_b[:], True, True,
                   T(xs, ident_b), T(ps))
            for cc in range(4):
                c = half * 4 + cc
                eng = "act" if cc % 2 == 0 else "dve"
                if eng == "act":
                    act(hT_out_fn(c), ps[:, cc * 128:(cc + 1) * 128], AF.Identity, T(ps, modc), fr["hT_w"],
                        scale=modc[:, m, gi, c:c + 1], bias=modc[:, m, si, c:c + 1])
                else:
                    ts("dve", hT_out_fn(c), ps[:, cc * 128:(cc + 1) * 128], modc[:, m, gi, c:c + 1],
                       modc[:, m, si, c:c + 1], ALU.mult, ALU.add, T(ps, modc), fr["hT_w"])

    def mixer_phase(l):
        last = (l == n_layers - 1)
        xA_d, xA_t, ob_d, ob_t = xA_ds[l], xA_ts[l], ob_ds[l], ob_ts[l]
        m0 = A.mark()
        w_in = A.alloc("w_in", [8, PROJ], BF16)
        w_out = A.alloc("w_out", [8, D], BF16)
        if l == 0:
            wiv = w_in_d[l].rearrange("(c p) n -> p c n", p=128)
            wov = w_out_d[l].rearrange("(c p) n -> p c n", p=128)
            for c in range(8):
                dma("pool", w_in[:, c, :], wiv[:, c, :], [], T(w_in))
            for c in range(0, 8, 4):
                dma("pool", w_out[:, c:c + 4, :], wov[:, c:c + 4, :], [], T(w_out))
        else:
            wiv = wbf[("w_in", l)].rearrange("(c p) n -> p c n", p=128)
            wov = wbf[("w_out", l)].rearrange("(c p) n -> p c n", p=128)
            for c in range(8):
                dma("sp", w_in[:, c, :], wiv[:, c, :], [wbf_t[("w_in", l)]], T(w_in))
            for c in range(0, 8, 4):
                dma("sp", w_out[:, c:c + 4, :], wov[:, c:c + 4, :], [wbf_t[("w_out", l)]], T(w_out))
        bg_dmas = []
        if l == 0:
            srcs = {"w_in": w_in_d, "w_out": w_out_d, "w_up": w_up_d, "w_dn": w_down_d}
            for key in (("w_up", 0), ("w_dn", 0), ("w_in", 1), ("w_out", 1), ("w_up", 1), ("w_dn", 1)):
                if key[1] >= n_layers:
                    continue
                dst = wbf[key]
                src_full = srcs[key[0]][key[1]]
                rows = dst.shape[0]
                step = 256
                for r0 in range(0, rows, step):
                    bg_dmas.append((dst[r0:r0 + step, :], src_full[r0:r0 + step, :], wbf_t[key]))
        S_f = A.ring("S_f", 2, [512], F32)
        S_b = A.ring("S_b", 2, [512], F32)
        g1_b = [A.alloc("g1b%d" % m, [1024], F32) for m in range(2)]
        for m in range(2):
            dma("sp", g1_b[m][:], ada_d[l, m:m + 1, 2048:3072].broadcast_to([128, 1024]), [ada_t[l]], T(g1_b[m]))
        hgb = A.alloc("hgb", [512], F32)
        lngb = A.alloc("lngb", [512], F32)
        lnbb = A.alloc("lnbb", [512], F32)
        bsb = A.alloc("bsb", [512], F32)
        wsn = A.alloc("wsn", [4, 128], BF16)
        wsT = A.alloc("wsT", [4, 128], BF16)
        dma("sp", hgb[:], hg_norm_d[l:l + 1, :].broadcast_to([128, 512]), [], T(hgb))
        dma("sp", lngb[:], sgu_g_d[l:l + 1, :].broadcast_to([128, 512]), [], T(lngb))
        dma("sp", lnbb[:], sgu_b_d[l:l + 1, :].broadcast_to([128, 512]), [], T(lnbb))
        dma("sp", bsb[:], b_sp_d[l:l + 1, :, :].rearrange("o h p -> o (h p)").broadcast_to([128, 512]), [], T(bsb))
        dma("pool", wsn[:], w_sp_d[l].rearrange("h p q -> p h q"), [], T(wsn))
        ps = PS.get()
        for h in range(4):
            mm(ps[:, h * 128:(h + 1) * 128], wsn[:, h, :], ident_b[:], True, True, T(wsn, ident_b), T(ps))
        cp("act", wsT[:].rearrange("p h q -> p (h q)"), ps[:], T(ps), T(wsT))

        st_r = A.ring("st", 3, [8], F32)
        junk = A.alloc("junk", [1024], BF16)
        xs_r = A.ring("xs", 2, [1024], BF16)
        xring = A.ring("xt", 2, [1024], F32)
        x2ring = A.ring("xt2", 2, [1024], F32)
        hTring = A.ring("hT", 2, [8, 128], BF16)
        tA = A.ring("tA", 6, [512], F32)
        bl_b = A.alloc("bl", [512], F32)
        eb_b = A.alloc("eb", [512], F32)
        KK_r = A.ring("KK", 2, [4, 4, 128], BF16)
        for b in KK_r.bufs:
            P.op("pool", lambda e, b=b: e.memset(b[:], 0.0), [], T(b))
        qdl_r = A.ring("qdl", 2, [512], BF16)
        qdt_r = A.ring("qdt", 3, [512], BF16)
        k2T_r = A.ring("k2T", 2, [512], BF16)
        k2t_r = A.ring("k2t", 2, [512], BF16)
        v_r = A.ring("vbf", 3, [512], BF16)
        scT_r = A.ring("scTb", 2, [512], BF16)
        ecol_r = A.ring("ecol", 3, [48], F32)
        Sbf_r = A.ring("Sbf", 2, [512], BF16)
        obs_r = A.ring("obs", 2, [512], F32)
        gs_r = A.ring("gs", 3, [512], F32)
        gu_r = A.ring("gu", 2, [512], F32)
        vn_r = A.ring("vn", 2, [512], BF16)
        tB = A.ring("tB", 2, [512], F32)
        col_r = A.ring("col", 6, [16], F32)
        ohg_r = A.ring("ohg", 2, [512], BF16)
        oTm_r = A.ring("oTm", 4, [4, 128], BF16)
        oTh_r = A.ring("oTh", 2, [4, 128], BF16)
        xn_r = A.ring("xn", 2, [1024], F32)

        def kk_view(KK, off, n):
            base = KK.ap
            return bass.AP(base.tensor, base.offset + off, [list(base.ap[0]), [128, 4], [544, n], [1, 32]])

        def v4(ap):
            return ap.rearrange("p (h c j) -> p h c j", h=4, c=4)

        def sigmoid_act(dst, src_ap, src_tiles):
            act(dst[:], src_ap, AF.Exp, src_tiles, T(dst), scale=-1.0)
            act(dst[:], dst[:], AF.Ln, T(dst), T(dst), bias=1.0)
            act(dst[:], dst[:], AF.Exp, T(dst), T(dst), scale=-1.0)

        def proj_fm(ps, hT, col0):
            for h in range(4):
                for c in range(8):
                    mm(ps[:, h * 128:(h + 1) * 128], w_in[:, c, col0 + h * 128:col0 + (h + 1) * 128], hT[:, c, :],
                       c == 0, c == 7, T(w_in, hT), T(ps))

        def proj_tm(ps, hT, col0):
            for c in range(8):
                mm(ps[:], hT[:, c, :], w_in[:, c, col0:col0 + 512], c == 0, c == 7, T(w_in, hT), T(ps))

        Sstate = {}

        def tile_gen(i, dirn, Sring):
            is_ctx = i < NT_CTX
            want_o = not (is_ctx and last)
            full = want_o and dirn == "f"
            m = 1 if is_ctx else 0
            di = 0 if dirn == "f" else 1
            edge = 31 if dirn == "f" else 0
            xt = xring.get()
            src, rd = src_tile_ap(l, i, "mix")
            dma("sp", xt[:], src, rd, T(xt))
            st = st_r.get()
            xs = xs_r.get()
            P.op("pool", lambda e: e.memset(st[:, 0:1], 0.0), [], T(st))
            act(junk[:], xt[:], AF.Square, T(xt), T(junk, st), accum_out=st[:, 0:1])
            rsqrt_cols(st[:, 2:3], st[:, 0:1], 1.0 / D, st, st[:, 1:2])
            act(xs[:], xt[:], AF.Identity, T(xt, st), T(xs), scale=st[:, 2:3])
            yield
            hT = hTring.get()
            for half in range(2):
                ps = PS.get()
                for cc in range(4):
                    c = half * 4 + cc
                    mm(ps[:, cc * 128:(cc + 1) * 128], xs[:, c * 128:(c + 1) * 128], ident_b[:], True, True,
                       T(xs, ident_b), T(ps))
                for cc in range(4):
                    c = half * 4 + cc
                    if cc % 2 == 0:
                        act(hT[:, c, :], ps[:, cc * 128:(cc + 1) * 128], AF.Identity, T(ps, modc), T(hT),
                            scale=modc[:, m, 0, c:c + 1], bias=modc[:, m, 1, c:c + 1])
                    else:
                        ts("dve", hT[:, c, :], ps[:, cc * 128:(cc + 1) * 128], modc[:, m, 0, c:c + 1],
                           modc[:, m, 1, c:c + 1], ALU.mult, ALU.add, T(ps, modc), T(hT))
            yield
            psF = PS.get()
            proj_fm(psF, hT, C_FF if dirn == "f" else C_FB)
            psV = PS.get()
            proj_tm(psV, hT, C_I)
            if full:
                psUu = PS.get()
                proj_fm(psUu, hT, C_U)
                psVm = PS.get()
                proj_tm(psVm, hT, C_V)
            if want_o:
                psQ = PS.get()
                proj_fm(psQ, hT, C_Q)
            if full:
                psG = PS.get()
                proj_tm(psG, hT, C_G)
            t1, t2, t3, t4, t5, t6 = tA.get(), tA.get(), tA.get(), tA.get(), tA.get(), tA.get()
            vbf = v_r.get()
            if full:
                gu = gu_r.get()
                act(gu[:], psUu[:], AF.Gelu, T(psUu), T(gu))
                act(t6[:], psVm[:], AF.Gelu, T(psVm), T(t6))
            act(t1[:], psF[:], AF.Exp, T(psF), T(t1), scale=-1.0)
            cp("act", vbf[:], psV[:], T(psV), T(vbf))
            if want_o:
                sigmoid_act(t4, psQ[:], T(psQ))
                tt("dve", t4[:], psQ[:], t4[:], ALU.mult, T(psQ, t4), T(t4))
            if full:
                gs = gs_r.get()
                sigmoid_act(gs, psG[:], T(psG))
                tt("dve", gs[:], psG[:], gs[:], ALU.mult, T(psG, gs), T(gs))
            act(t1[:], t1[:], AF.Ln, T(t1), T(t1), bias=1.0)
            act(t2[:], t1[:], AF.Exp, T(t1), T(t2), scale=-1.0)
            if l > 0:
                t23 = t2[:].rearrange("p (h j) -> p h j", h=4)
                for h in range(4):
                    ts("dve", t23[:, h, :], t23[:, h, :], lbc[:, di * 8 + 4 + h:di * 8 + 5 + h],
                       lbc[:, di * 8 + h:di * 8 + h + 1], ALU.mult, ALU.add, T(t2, lbc), T(t2))
                act(t1[:], t2[:], AF.Ln, T(t2), T(t1))
            act(t2[:], t2[:], AF.Identity, T(t2), T(t2), scale=-1.0, bias=1.0)
            sop = ALU.subtract if l == 0 else ALU.add
            P.op("dve", lambda e: e.tensor_tensor_scan(bl_b[:], scanmask, t1[:], 0.0, ALU.mult, sop),
                 T(cst, t1), T(bl_b))
            Bsrc = bl_b
            if dirn == "b":
                tt("dve", t1[:], bl_b[:], t1[:], ALU.add if l == 0 else ALU.subtract, T(bl_b, t1), T(t1))
                tt("dve", v4(t3[:]), v4(bl_b[:])[:, :, :, 31:32].broadcast_to([128, 4, 4, 32]), v4(t1[:]),
                   ALU.subtract, T(bl_b, t1), T(t3))
                Bsrc = t3
            eb = eb_b
            act(eb[:], Bsrc[:], AF.Exp, T(Bsrc), T(eb))
            act(t1[:], Bsrc[:], AF.Exp, T(Bsrc), T(t1), scale=-1.0)
            eb4 = v4(eb[:])
            tt("dve", t1[:], t2[:], t1[:], ALU.mult, T(t2, t1), T(t1))
            tt("dve", v4(t5[:]), v4(t1[:]), eb4[:, :, :, edge:edge + 1].broadcast_to([128, 4, 4, 32]), ALU.mult,
               T(t1, eb), T(t5))
            ecol = ecol_r.get()
            Plo = ecol[:, 0:16].rearrange("p (h c) -> p h c", c=4)
            Phi = ecol[:, 16:32].rearrange("p (h c) -> p h c", c=4)
            Etile = ecol[:, 32:36]
            e12 = ecol[:, 36:40]
            ec = [eb4[:, :, c, edge] for c in range(4)]
            P.op("pool", lambda e: e.memset(ecol[:, 0:32], 1.0), [], T(ecol))
            cp("pool", Plo[:, :, 1], ec[0], T(eb), T(ecol))
            tt("pool", Plo[:, :, 2], Plo[:, :, 1], ec[1], ALU.mult, T(eb, ecol), T(ecol))
            tt("pool", Plo[:, :, 3], Plo[:, :, 2], ec[2], ALU.mult, T(eb, ecol), T(ecol))
            cp("pool", Phi[:, :, 2], ec[3], T(eb), T(ecol))
            tt("pool", Phi[:, :, 1], Phi[:, :, 2], ec[2], ALU.mult, T(eb, ecol), T(ecol))
            tt("pool", Phi[:, :, 0], Phi[:, :, 1], ec[1], ALU.mult, T(eb, ecol), T(ecol))
            tt("pool", Etile, Plo[:, :, 3], ec[3], ALU.mult, T(eb, ecol), T(ecol))
            tt("pool", e12, ec[1], ec[2], ALU.mult, T(eb), T(ecol))
            Ecum, Esuf = (Plo, Phi) if dirn == "f" else (Phi, Plo)
            k2T = k2T_r.get()
            tt("dve", v4(k2T[:]), v4(t5[:]), Esuf.rearrange("p h (c o) -> p h c o", o=1).broadcast_to([128, 4, 4, 32]),
               ALU.mult, T(t5, ecol), T(k2T))
            if want_o:
                KK = KK_r.get()
                k5 = v4(t5[:])
                e_mid = eb4[:, :, 1:3, edge:edge + 1].broadcast_to([128, 4, 2, 32])
                e12b = e12.rearrange("p (h a b) -> p h a b", a=1, b=1).broadcast_to([128, 4, 1, 32])
                cp("pool", kk_view(KK, 0, 4), v4(t1[:]), T(t1), T(KK))
                if dirn == "f":
                    cp("pool", kk_view(KK, 512, 3), k5[:, :, 0:3, :], T(t5), T(KK))
                    tt("pool", kk_view(KK, 1024, 2), k5[:, :, 0:2, :], e_mid, ALU.mult, T(t5, eb), T(KK))
                    tt("pool", kk_view(KK, 1536, 1), k5[:, :, 0:1, :], e12b, ALU.mult, T(t5, ecol), T(KK))
                else:
                    cp("pool", kk_view(KK, 32, 3), k5[:, :, 1:4, :], T(t5), T(KK))
                    tt("pool", kk_view(KK, 64, 2), k5[:, :, 2:4, :], e_mid, ALU.mult, T(t5, eb), T(KK))
                    tt("pool", kk_view(KK, 96, 1), k5[:, :, 3:4, :], e12b, ALU.mult, T(t5, ecol), T(KK))
                stt("dve", t4[:], t4[:], float(128 ** -0.5), eb[:], ALU.mult, ALU.mult, T(t4, eb), T(t4))
                qdl = qdl_r.get()
                cp("dve", qdl[:], t4[:], T(t4), T(qdl))
                qdt = qdt_r.get()
                tt("dve", v4(qdt[:]), v4(t4[:]), Ecum.rearrange("p h (c o) -> p h c o", o=1).broadcast_to([128, 4, 4, 32]),
                   ALU.mult, T(t4, ecol), T(qdt))
            if full:
                gv3 = t6[:].rearrange("p (h v) -> p h v", h=4)
                col2 = col_r.get()
                P.op("dve", lambda e: e.tensor_reduce(col2[:, 0:4], t6[:].rearrange("p (h v) -> p h v", h=4),
                                                      AX.X, ALU.add), T(t6), T(col2))
                ts("dve", col2[:, 0:4], col2[:, 0:4], 1.0 / 128, None, ALU.mult, None, T(col2), T(col2))
                tt("dve", gv3, gv3, col2[:, 0:4].rearrange("p (h o) -> p h o", o=1).broadcast_to([128, 4, 128]),
                   ALU.subtract, T(t6, col2), T(t6))
                act(t2[:], t6[:], AF.Square, T(t6), T(t2))
                P.op("dve", lambda e: e.tensor_reduce(col2[:, 4:8], t2[:].rearrange("p (h v) -> p h v", h=4),
                                                      AX.X, ALU.add), T(t2), T(col2))
                rsqrt_cols(col2[:, 12:16], col2[:, 4:8], 1.0 / 128, col2, col2[:, 8:12])
                tt("dve", gv3, gv3, col2[:, 12:16].rearrange("p (h o) -> p h o", o=1).broadcast_to([128, 4, 128]),
                   ALU.mult, T(t6, col2), T(t6))
                tt("pool", t6[:], t6[:], lngb[:], ALU.mult, T(t6, lngb), T(t6))
                vn = vn_r.get()
                tt("pool", vn[:], t6[:], lnbb[:], ALU.add, T(t6, lnbb), T(vn))
            yield
            psK = PS.get()
            for h in range(4):
                mm(psK[:, h * 128:(h + 1) * 128], k2T[:, h * 128:(h + 1) * 128], ident_b[:], True, True,
                   T(k2T, ident_b), T(psK))
            k2t = k2t_r.get()
            cp("act", k2t[:], psK[:], T(psK), T(k2t))
            if want_o:
                psS = PS.get()
                for ct in range(4):
                    for h in range(4):
                        mm(psS[:, h * 128 + ct * 32:h * 128 + ct * 32 + 32], KK[:, ct, h, :],
                           qdl[:, h * 128 + ct * 32:h * 128 + ct * 32 + 32], True, True, T(KK, qdl), T(psS))
                scTb = scT_r.get()
                mk = maskF if dirn == "f" else maskB
                tt("dve", scTb[:].rearrange("p (h t) -> p h t", h=4), psS[:].rearrange("p (h t) -> p h t", h=4),
                   mk.rearrange("p (o t) -> p o t", o=1).broadcast_to([128, 4, 128]), ALU.mult, T(psS, cst), T(scTb))
            if full:
                psZ = PS.get()
                for h in range(4):
                    mm(psZ[:, h * 128:(h + 1) * 128], vn[:, h * 128:(h + 1) * 128], wsT[:, h, :], True, True,
                       T(vn, wsT), T(psZ))
                oTm = oTm_r.get()
                tz = tB.get()
                tt("dve", tz[:], psZ[:], bsb[:], ALU.add, T(psZ, bsb), T(tz))
                tt("dve", oTm[:].rearrange("p h t -> p (h t)"), tz[:], gu[:], ALU.mult, T(tz, gu), T(oTm))
                obs = obs_r.get()
                dma("sp", obs[:], ob_d[i * 128:(i + 1) * 128, :], [ob_t[i]], T(obs))
            yield
            S_old = Sring.bufs[Sstate[dirn] % 2]
            S_new = Sring.bufs[(Sstate[dirn] + 1) % 2]
            Sstate[dirn] += 1
            if want_o:
                Sbf = Sstate[dirn + "bf"]
                psO = PS.get()
                for h in range(4):
                    mm(psO[:, h * 128:(h + 1) * 128], scTb[:, h * 128:(h + 1) * 128], vbf[:, h * 128:(h + 1) * 128],
                       h == 0, False, T(scTb, vbf), T(psO), skip=True)
                for h in range(4):
                    mm(psO[:, h * 128:(h + 1) * 128], qdt[:, h * 128:(h + 1) * 128], Sbf[:, h * 128:(h + 1) * 128],
                       False, True, T(qdt, Sbf), T(psO), skip=True)
            psU = PS.get()
            for h in range(4):
                mm(psU[:, h * 128:(h + 1) * 128], k2t[:, h * 128:(h + 1) * 128], vbf[:, h * 128:(h + 1) * 128],
                   True, True, T(k2t, vbf), T(psU))
            for h in range(4):
                stt("dve", S_new[:, h * 128:(h + 1) * 128], S_old[:, h * 128:(h + 1) * 128], Etile[:, h:h + 1],
                    psU[:, h * 128:(h + 1) * 128], ALU.mult, ALU.add, T(S_old, ecol, psU), T(S_new))
            Sbf_n = Sbf_r.get()
            cp("act", Sbf_n[:], S_new[:], T(S_new), T(Sbf_n))
            Sstate[dirn + "bf"] = Sbf_n
            if not want_o:
                return
            if dirn == "b":
                obs = obs_r.get()
                cp("act", obs[:], psO[:], T(psO), T(obs))
                dma("pool", ob_d[i * 128:(i + 1) * 128, :], obs[:], T(obs), [ob_t[i]])
                return
            osum, osq = tB.get(), tB.get()
            col = col_r.get()
            tt("dve", osum[:], psO[:], obs[:], ALU.add, T(psO, obs), T(osum))
            act(osq[:], osum[:], AF.Square, T(osum), T(osq))
            P.op("dve", lambda e: e.tensor_reduce(col[:, 0:4], osq[:].rearrange("p (h v) -> p h v", h=4),
                                                  AX.X, ALU.add), T(osq), T(col))
            rsqrt_cols(col[:, 8:12], col[:, 0:4], 1.0 / 128, col, col[:, 4:8])
            o3 = osum[:].rearrange("p (h v) -> p h v", h=4)
            tt("dve", o3, o3, col[:, 8:12].rearrange("p (h o) -> p h o", o=1).broadcast_to([128, 4, 128]), ALU.mult,
               T(osum, col), T(osum))
            tt("pool", osum[:], osum[:], hgb[:], ALU.mult, T(osum, hgb), T(osum))
            ohg = ohg_r.get()
            tt("dve", ohg[:], osum[:], gs[:], ALU.mult, T(osum, gs), T(ohg))
            yield
            psT2 = PS.get()
            for h in range(4):
                mm(psT2[:, h * 128:(h + 1) * 128], ohg[:, h * 128:(h + 1) * 128], ident_b[:], True, True,
                   T(ohg, ident_b), T(psT2))
            oTh = oTh_r.get()
            cp("act", oTh[:].rearrange("p h t -> p (h t)"), psT2[:], T(psT2), T(oTh))
            xt2 = x2ring.get()
            dma("sp", xt2[:], src, rd, T(xt2))
            yield
            xn = xn_r.get()
            for n in range(2):
                psW = PS.get()
                for c in range(8):
                    lhs = oTh[:, c, :] if c < 4 else oTm[:, c - 4, :]
                    mm(psW[:], lhs, w_out[:, c, n * 512:(n + 1) * 512], c == 0, c == 7, T(oTh, oTm, w_out), T(psW))
                tt("dve", xn[:, n * 512:(n + 1) * 512], psW[:], g1_b[m][:, n * 512:(n + 1) * 512], ALU.mult,
                   T(psW, g1_b[m]), T(xn))
            tt("pool", xn[:], xn[:], xt2[:], ALU.add, T(xn, xt2), T(xn))
            dma("pool", xA_d[i * 128:(i + 1) * 128, :], xn[:], T(xn), [xA_t[i]])

        def run_pass(order, dirn, Sring):
            P.op("pool", lambda e: e.memset(Sring.bufs[0][:], 0.0), [], T(Sring.bufs[0]))
            Sstate[dirn] = 0
            Sbf0 = Sbf_r.get()
            P.op("pool", lambda e: e.memset(Sbf0[:], 0.0), [], T(Sbf0))
            Sstate[dirn + "bf"] = Sbf0
            live = []
            pending = list(order)
            while pending or live:
                for g in list(live):
                    try:
                        next(g)
                    except StopIteration:
                        live.remove(g)
                if pending:
                    g = tile_gen(pending.pop(0), dirn, Sring)
                    next(g)
                    live.append(g)
                if bg_dmas and len(live) >= 3:
                    o_ap, i_ap, wt = bg_dmas.pop(0)
                    dma("pool", o_ap, i_ap, [], [wt])

        run_pass([1, 0] + list(range(NT - 1, NT_CTX - 1, -1)), "b", S_b)
        run_pass(list(range(NT)), "f", S_f)
        while bg_dmas:
            o_ap, i_ap, wt = bg_dmas.pop(0)
            dma("pool", o_ap, i_ap, [], [wt])
        A.release(m0)


    def ffn_phase(l):
        last = (l == n_layers - 1)
        xA_d, xA_t, xB_d, xB_t = xA_ds[l], xA_ts[l], xB_ds[l], xB_ts[l]
        m0 = A.mark()
        w_up = A.alloc("w_up", [8, 2 * DFF], BF16)
        w_dn = A.alloc("w_dn", [NFC, D], BF16)
        wuv = wbf[("w_up", l)].rearrange("(c p) n -> p c n", p=128)
        wdv = wbf[("w_dn", l)].rearrange("(c p) n -> p c n", p=128)
        for c in range(8):
            for hlf in range(2):
                dma("sp", w_up[:, c, hlf * DFF:(hlf + 1) * DFF], wuv[:, c, hlf * DFF:(hlf + 1) * DFF],
                    [wbf_t[("w_up", l)]], T(w_up))
        for c in range(0, NFC, 2):
            dma("sp", w_dn[:, c:c + 2, :], wdv[:, c:c + 2, :], [wbf_t[("w_dn", l)]], T(w_dn))
        cw = A.alloc("cw", [NFC, 9], F32)
        cb = A.alloc("cb", [NFC], F32)
        for t9 in range(9):
            dma("sp", cw[:, :, t9], conv_w_d[l, t9 // 3, t9 % 3, :].rearrange("(c p) -> p c", p=128), [], T(cw),
                allow_slow_non_contiguous=True)
        dma("sp", cb[:], conv_b_d[l].rearrange("(c p) -> p c", p=128), [], T(cb), allow_slow_non_contiguous=True)
        g2_b = [A.alloc("g2b%d" % m, [1024], F32) for m in range(2)]
        for m in range(2):
            dma("sp", g2_b[m][:], ada_d[l, m:m + 1, 5120:6144].broadcast_to([128, 1024]), [ada_t[l]], T(g2_b[m]))
        nfb = None
        if last:
            nfb = A.alloc("nfb", [1024], F32)
            dma("sp", nfb[:], norm_final_d.rearrange("(o n) -> o n", o=1).broadcast_to([128, 1024]), [], T(nfb))

        fr = {
            "st": A.ring("fst", 2, [8], F32),
            "junk": None,
            "xs": A.ring("fxs", 1, [1024], BF16),
        }
        xring = A.ring("fxt", 2, [1024], F32)
        h2T = A.alloc("h2T", [8, 640], BF16)
        mT = A.alloc("mT", [NFC, 512], BF16)
        Gsb_r = A.ring("Gsb", 2, [640], F32)
        acc_r = A.ring("acc", 2, [512], F32)
        xs0 = fr["xs"].bufs[0]
        asb_r = Ring([A.alloc("asb0", [512], F32), Buf(xs0.ap.bitcast(F32), xs0.t)])
        xn_r = A.ring("fxn", 1, [1024], F32)
        fcol_r = A.ring("fcol", 2, [8], F32)

        def block(kind, b):
            if kind == "lat":
                m = 0
                tiles = [2 + 4 * b + k for k in range(4)]
                ntok = 512
                W, nrows, ro = 64, 8, 1
                has_prev, has_next = b > 0, b < 7
                taps = [(ky, kx) for ky in range(3) for kx in range(3)]
            else:
                m = 1
                tiles = [0, 1]
                ntok = 256
                W, nrows, ro = 256, 1, 0
                has_prev = has_next = False
                taps = [(1, kx) for kx in range(3)]
            halo = kind == "lat"
            fr["hT_w"] = T(h2T)
            for k, i in enumerate(tiles):
                xt = xring.get()
                src, rd = src_tile_ap(l, i, "ffn")
                dma("sp", xt[:], src, rd, T(xt))
                norm_transpose(xt, m, "ffn", lambda c, k=k: h2T[:, c, k * 128:(k + 1) * 128], fr)
                yield "F"
            if halo:
                xt = xring.get()
                t_first, t_last = tiles[0], tiles[-1]
                ip = t_first - 1 if has_prev else t_first
                inx = t_last + 1 if has_next else t_last
                dma("sp", xt[0:64, :], xA_d[ip * 128 + 64:ip * 128 + 128, :], [xA_t[ip]], T(xt))
                dma("sp", xt[64:128, :], xA_d[inx * 128:inx * 128 + 64, :], [xA_t[inx]], T(xt))
                norm_transpose(xt, m, "ffn", lambda c: h2T[:, c, 512:640], fr)
            yield "L"
            pending = None
            for fc in range(NFC):
                psA = PS.get()
                for c in range(8):
                    mm(psA[:, 0:ntok], w_up[:, c, fc * 128:(fc + 1) * 128], h2T[:, c, 0:ntok], c == 0, c == 7,
                       T(w_up, h2T), T(psA))
                asb = asb_r.get()
                cp("act", asb[:, 0:ntok], psA[:, 0:ntok], T(psA), T(asb))
                psG = PS.get()
                for c in range(8):
                    mm(psG[:, 0:ntok], w_up[:, c, DFF + fc * 128:DFF + (fc + 1) * 128], h2T[:, c, 0:ntok], c == 0, c == 7,
                       T(w_up, h2T), T(psG))
                Gsb = Gsb_r.get()
                if halo:
                    psH = PS.get()
                    for c in range(8):
                        mm(psH[:, 0:128], w_up[:, c, DFF + fc * 128:DFF + (fc + 1) * 128], h2T[:, c, 512:640], c == 0,
                           c == 7, T(w_up, h2T), T(psH))
                    G3 = Gsb[:].rearrange("p (r w) -> p r w", w=64)
                    cp("act", Gsb[:, 64:576], psG[:, 0:512], T(psG), T(Gsb))
                    if has_prev:
                        cp("act", Gsb[:, 0:64], psH[:, 0:64], T(psH), T(Gsb))
                    else:
                        P.op("pool", lambda e, Gsb=Gsb: e.memset(Gsb[:, 0:64], 0.0), [], T(Gsb))
                    if has_next:
                        cp("act", Gsb[:, 576:640], psH[:, 64:128], T(psH), T(Gsb))
                    else:
                        P.op("pool", lambda e, Gsb=Gsb: e.memset(Gsb[:, 576:640], 0.0), [], T(Gsb))
                else:
                    G3 = Gsb[:, 0:ntok].rearrange("p (r w) -> p r w", w=W)
                    cp("act", Gsb[:, 0:ntok], psG[:, 0:ntok], T(psG), T(Gsb))
                acc = acc_r.get()
                a3 = acc[:, 0:ntok].rearrange("p (r w) -> p r w", w=W)
                act(acc[:, 0:ntok].rearrange("p (r w) -> p r w", w=W), G3[:, ro:ro + nrows, :], AF.Identity,
                    T(Gsb, cw, cb), T(acc), scale=cw[:, fc, 4:5], bias=cb[:, fc:fc + 1])
                if pending is not None:
                    act(pending[2][:, 0:ntok], pending[2][:, 0:ntok], AF.Gelu, T(pending[2]), T(pending[2]))
                    pf, pA, pacc = pending
                    tt("dve", mT[:, pf, 0:ntok], pA[:, 0:ntok], pacc[:, 0:ntok], ALU.mult, T(pA, pacc), T(mT))
                    pending = None
                for (ky, kx) in taps:
                    if ky == 1 and kx == 1:
                        continue
                    r_src = ro + ky - 1
                    if kx == 0:
                        o_ap, i_ap = a3[:, :, 1:W], G3[:, r_src:r_src + nrows, 0:W - 1]
                    elif kx == 1:
                        o_ap, i_ap = a3[:, :, :], G3[:, r_src:r_src + nrows, :]
                    else:
                        o_ap, i_ap = a3[:, :, 0:W - 1], G3[:, r_src:r_src + nrows, 1:W]
                    t9 = ky * 3 + kx
                    stt("dve", o_ap, i_ap, cw[:, fc, t9:t9 + 1], o_ap, ALU.mult, ALU.add, T(Gsb, cw, acc), T(acc))
                if pending is not None:
                    pf, pA, pacc = pending
                    tt("dve", mT[:, pf, 0:ntok], pA[:, 0:ntok], pacc[:, 0:ntok], ALU.mult, T(pA, pacc), T(mT))
                pending = (fc, asb, acc)
            pf, pA, pacc = pending
            act(pacc[:, 0:ntok], pacc[:, 0:ntok], AF.Gelu, T(pacc), T(pacc))
            tt("dve", mT[:, pf, 0:ntok], pA[:, 0:ntok], pacc[:, 0:ntok], ALU.mult, T(pA, pacc), T(mT))
            for k, i in enumerate(tiles):
                xt = xring.get()
                src, rd = src_tile_ap(l, i, "ffn")
                dma("sp", xt[:], src, rd, T(xt))
                xn = xn_r.get()
                for n in range(2):
                    psW = PS.get()
                    for fc in range(NFC):
                        mm(psW[:], mT[:, fc, k * 128:(k + 1) * 128], w_dn[:, fc, n * 512:(n + 1) * 512], fc == 0,
                           fc == NFC - 1, T(mT, w_dn), T(psW))
                    tt("dve", xn[:, n * 512:(n + 1) * 512], psW[:], g2_b[m][:, n * 512:(n + 1) * 512], ALU.mult,
                       T(psW, g2_b[m]), T(xn))
                tt("pool", xn[:], xn[:], xt[:], ALU.add, T(xn, xt), T(xn))
                if last:
                    col = fcol_r.get()
                    P.op("pool", lambda e, col=col: e.memset(col[:, 0:1], 0.0), [], T(col))
                    jk = fr["xs"].bufs[0]
                    act(jk[:], xn[:], AF.Square, T(xn), T(jk, col), accum_out=col[:, 0:1])
                    rsqrt_cols(col[:, 2:3], col[:, 0:1], 1.0 / D, col, col[:, 1:2])
                    stt("dve", xn[:], xn[:], col[:, 2:3], nfb[:], ALU.mult, ALU.mult, T(xn, col, nfb), T(xn))
                    j = i - 2
                    dma("pool", out_d[j * 128:(j + 1) * 128, :], xn[:], T(xn), [out_t[j]])
                else:
                    dma("pool", xB_d[i * 128:(i + 1) * 128, :], xn[:], T(xn), [xB_t[i]])
                yield "D"

        def step(g):
            try:
                return next(g)
            except StopIteration:
                return None

        gens = ([] if last else [block("ctx", 0)]) + [block("lat", b) for b in range(8)]
        while step(gens[0]) != "L":
            pass
        for bi, g in enumerate(gens):
            nxt = gens[bi + 1] if bi + 1 < len(gens) else None
            g_alive, n_at_L = True, nxt is None
            while g_alive or not n_at_L:
                if g_alive and step(g) is None:
                    g_alive = False
                if not n_at_L and step(nxt) == "L":
                    n_at_L = True
        A.release(m0)

    prologue()
    for l in range(n_layers):
        layer_consts(l)
        mixer_phase(l)
        ffn_phase(l)
    stats = P.emit()
    es.close()
    return nc, stats, A.peak


_CACHE = {}


def kernel(_debug=False, **inputs):
    key = "nc_dbg" if _debug else "nc"
    if key not in _CACHE:
        _CACHE[key] = build_program(debug=_debug)[0]
    nc = _CACHE[key]
    f = lambda a: np.ascontiguousarray(np.asarray(a, dtype=np.float32))
    x = f(inputs["x"])
    c = f(inputs["c"])
    ctx = f(inputs["ctx"])
    c_ctx = f(inputs["c_ctx"])
    consts = host_consts()
    shared = {k: f(inputs[k]) for k in (
        "w_ada", "b_ada", "norm_mix", "norm_ffn", "w_in", "lb_logits_fwd", "lb_logits_bwd", "hg_norm",
        "sgu_norm_g", "sgu_norm_b", "w_spatial", "b_spatial", "w_out", "w_up", "conv_w", "conv_b", "w_down",
        "norm_final")}
    in_maps = []
    for b in range(8):
        cc = np.stack([c[b], c_ctx], axis=-1).reshape(8, 128, 2).transpose(1, 0, 2)
        d = {"x": x[b], "ctx": ctx[b], "cc": np.ascontiguousarray(cc), "consts": consts}
        d.update(shared)
        in_maps.append(d)
    res = run_bass_kernel_spmd(nc, in_maps, core_ids=list(range(8)))
    if _debug:
        return res.results
    return np.stack([np.asarray(res.results[b]["out"], dtype=np.float32) for b in range(8)], axis=0)


if __name__ == "__main__":
    import time
    t0 = time.time()
    nc, stats, peak = build_program()
    print("build", time.time() - t0, stats, "arena peak", peak)
```

```python
import numpy as np
from collections import deque
from contextlib import ExitStack

import concourse.bass as bass
import concourse.mybir as mybir
from concourse.bass_utils import run_bass_kernel_spmd

F32 = mybir.dt.float32
BF16 = mybir.dt.bfloat16
AF = mybir.ActivationFunctionType
ALU = mybir.AluOpType
AX = mybir.AxisListType

COMPUTE = ("pe", "act", "dve", "pool")
ENGS = ("pe", "act", "dve", "pool", "sp")

D = 1024
T_LAT = 4096
T_CTX = 256
NT_LAT = 32
NT_CTX = 2
NT = 34
HW = 512
PROJ = 3584
DFF = 2816
NFC = 22
EPS = 1e-6
C_FF, C_FB, C_I, C_Q, C_G, C_U, C_V = 0, 512, 1024, 1536, 2048, 2560, 3072


class Tile:
    __slots__ = ("name", "writers", "readers")

    def __init__(self, name, init=None):
        self.name = name
        self.writers = list(init) if init else []
        self.readers = []


class Op:
    __slots__ = ("eng", "fn", "deps", "inc", "val", "dma", "sem", "prev_dma")

    def __init__(self, eng, fn, dma):
        self.eng = eng
        self.fn = fn
        self.deps = []
        self.inc = False
        self.val = 0
        self.dma = dma
        self.sem = None
        self.prev_dma = None


class Prog:
    def __init__(self, nc, n_dma_sems=16):
        self.nc = nc
        self.ops = []
        self.n_dma_sems = n_dma_sems
        self.last_op = {}
        self.recent_dma = {e: deque(maxlen=n_dma_sems) for e in ENGS}

    def frontier(self):
        f = [o for o in self.last_op.values()]
        for e in ENGS:
            f.extend(self.recent_dma[e])
        return f

    def op(self, eng, fn, reads=(), writes=(), dma=False):
        o = Op(eng, fn, dma)
        deps = {}
        for t in reads:
            for w in t.writers:
                if (not dma) and (not w.dma) and w.eng == eng and eng == "pe":
                    continue
                deps[id(w)] = w
        for t in writes:
            for w in t.writers:
                if (not dma) and (not w.dma) and w.eng == eng:
                    continue
                deps[id(w)] = w
            for r in t.readers:
                if (not dma) and (not r.dma) and r.eng == eng:
                    continue
                deps[id(r)] = r
        for t in reads:
            t.readers.append(o)
        for t in writes:
            t.writers = [o]
            t.readers = []
        o.deps = list(deps.values())
        for d in o.deps:
            d.inc = True
        self.ops.append(o)
        if dma:
            self.recent_dma[eng].append(o)
        else:
            self.last_op[eng] = o
        return o

    def emit(self):
        nc = self.nc
        by_eng = {e: [] for e in ENGS}
        for o in self.ops:
            by_eng[o.eng].append(o)
        stats = {}
        with ExitStack() as es:
            esem = {e: es.enter_context(nc.semaphore("s_" + e)) for e in COMPUTE}
            dsem = {}
            for e in ENGS:
                if any(o.dma for o in by_eng[e]):
                    dsem[e] = [es.enter_context(nc.semaphore("d_%s_%d" % (e, i)))
                               for i in range(self.n_dma_sems)]
            for e in ENGS:
                cnt = 0
                dcnt = [0] * self.n_dma_sems
                last_on_slot = [None] * self.n_dma_sems
                k = 0
                for o in by_eng[e]:
                    if o.dma:
                        j = k % self.n_dma_sems
                        k += 1
                        dcnt[j] += 16
                        o.sem = dsem[e][j]
                        o.val = dcnt[j]
                        o.prev_dma = last_on_slot[j]
                        last_on_slot[j] = o
                    else:
                        if o.inc:
                            cnt += 1
                        o.sem = esem[e]
                        o.val = cnt
            block = es.enter_context(nc.Block())

            def run_engine(e, engobj):
                known = {}
                nwait = [0]

                def wait(sem, val):
                    if val <= 0:
                        return
                    key = sem.num
                    if known.get(key, 0) >= val:
                        return
                    engobj.wait_ge(sem, val)
                    known[key] = val
                    nwait[0] += 1

                for o in by_eng[e]:
                    for d in o.deps:
                        wait(d.sem, d.val)
                    if o.dma and o.prev_dma is not None:
                        wait(o.prev_dma.sem, o.prev_dma.val)
                    ins = o.fn(engobj)
                    if o.dma:
                        ins.then_inc(o.sem, 16)
                    elif o.inc:
                        ins.then_inc(o.sem, 1)
                if e in dsem:
                    last = {}
                    for o in by_eng[e]:
                        if o.dma:
                            last[o.sem.num] = o
                    for o in last.values():
                        wait(o.sem, o.val)
                stats[e] = (len(by_eng[e]), nwait[0])

            @block.sync
            def _(eng):
                run_engine("sp", eng)

            @block.tensor
            def _(eng):
                run_engine("pe", eng)

            @block.scalar
            def _(eng):
                run_engine("act", eng)

            @block.vector
            def _(eng):
                run_engine("dve", eng)

            @block.gpsimd
            def _(eng):
                run_engine("pool", eng)
        return stats


class Buf:
    __slots__ = ("ap", "t")

    def __init__(self, ap, t):
        self.ap = ap
        self.t = t

    def __getitem__(self, idx):
        return self.ap[idx]


class Ring:
    def __init__(self, bufs):
        self.bufs = bufs
        self.i = 0

    def get(self):
        b = self.bufs[self.i % len(self.bufs)]
        self.i += 1
        return b


class Arena:
    def __init__(self, P, arena_ap, nbytes):
        self.P = P
        self.a = arena_ap
        self.n = nbytes
        self.off = 0
        self.reused = False
        self.peak = 0

    def mark(self):
        return self.off

    def release(self, m):
        self.off = m
        self.reused = True

    def alloc(self, name, free_shape, dtype, parts=128):
        esz = 4 if dtype == F32 else 2
        n = 1
        for s in free_shape:
            n *= s
        nb = (n * esz + 63) // 64 * 64
        assert self.off + nb <= self.n, "arena overflow at %s: %d + %d > %d" % (name, self.off, nb, self.n)
        ap = self.a[0:parts, self.off // 2:self.off // 2 + (n * esz) // 2]
        if dtype == F32:
            ap = ap.bitcast(F32)
        if len(free_shape) == 2:
            ap = ap.rearrange("p (a b) -> p a b", b=free_shape[1])
        elif len(free_shape) == 3:
            ap = ap.rearrange("p (a b c) -> p a b c", b=free_shape[1], c=free_shape[2])
        self.off += nb
        self.peak = max(self.peak, self.off)
        t = Tile(name, self.P.frontier() if self.reused else None)
        return Buf(ap, t)

    def ring(self, name, n, free_shape, dtype):
        return Ring([self.alloc("%s%d" % (name, i), free_shape, dtype) for i in range(n)])


def T(*bufs):
    return [b.t for b in bufs]


def host_consts():
    c = np.zeros((128, 1024), np.float32)
    c[:, 0:128] = np.eye(128, dtype=np.float32)
    s = np.arange(128)[:, None]
    t = np.arange(128)[None, :]
    same = (s // 32) == (t // 32)
    c[:, 128:256] = (s <= t).astype(np.float32)
    c[:, 256:384] = (s >= t).astype(np.float32)
    j = np.arange(512)
    c[:, 384:896] = (j % 32 != 0).astype(np.float32)[None, :]
    for cc in range(4):
        c[:, 896 + cc] = ((np.arange(128) // 32) == cc).astype(np.float32)
    c[:, 900] = 1.0
    return c


def build_program(n_layers=2, debug=False):
    nc = bass.Bass("TRN2", target_bir_lowering=False)
    es = ExitStack()

    def din(name, shape):
        return nc.dram_tensor(name, list(shape), F32, kind="ExternalInput").ap()

    x_d = din("x", [T_LAT, D])
    ctx_d = din("ctx", [T_CTX, D])
    cc_d = din("cc", [128, 8, 2])
    consts_d = din("consts", [128, 1024])
    w_ada_d = din("w_ada", [2, D, 6 * D])
    b_ada_d = din("b_ada", [2, 6 * D])
    norm_mix_d = din("norm_mix", [2, D])
    norm_ffn_d = din("norm_ffn", [2, D])
    w_in_d = din("w_in", [2, D, PROJ])
    lbf_d = din("lb_logits_fwd", [2, HW])
    lbb_d = din("lb_logits_bwd", [2, HW])
    hg_norm_d = din("hg_norm", [2, HW])
    sgu_g_d = din("sgu_norm_g", [2, HW])
    sgu_b_d = din("sgu_norm_b", [2, HW])
    w_sp_d = din("w_spatial", [2, 4, 128, 128])
    b_sp_d = din("b_spatial", [2, 4, 128])
    w_out_d = din("w_out", [2, D, D])
    w_up_d = din("w_up", [2, D, 2 * DFF])
    conv_w_d = din("conv_w", [2, 3, 3, DFF])
    conv_b_d = din("conv_b", [2, DFF])
    w_down_d = din("w_down", [2, DFF, D])
    norm_final_d = din("norm_final", [D])
    out_d = nc.dram_tensor("out", [T_LAT, D], F32, kind="ExternalOutput").ap()

    if debug:
        kw = dict(kind="ExternalOutput")
        xA_ds = [nc.dram_tensor("xA%d" % l, [NT * 128, D], F32, **kw).ap() for l in range(2)]
        xB_ds = [nc.dram_tensor("xB%d" % l, [NT * 128, D], F32, **kw).ap() for l in range(2)]
        ob_ds = [nc.dram_tensor("obs%d" % l, [NT * 128, HW], F32, **kw).ap() for l in range(2)]
        ada_d = nc.dram_tensor("ada_s", [2, 2, 6 * D], F32, **kw).ap()
    else:
        xA_ds = [nc.dram_tensor("xA", [NT * 128, D], F32).ap()] * 2
        xB_ds = [nc.dram_tensor("xB", [NT * 128, D], F32).ap()] * 2
        ob_ds = [nc.dram_tensor("obs", [NT * 128, HW], F32).ap()] * 2
        ada_d = nc.dram_tensor("ada_s", [2, 2, 6 * D], F32).ap()

    wbf = {}
    for l_ in range(2):
        if l_ > 0:
            wbf[("w_in", l_)] = nc.dram_tensor("w_in_bf%d" % l_, [D, PROJ], BF16).ap()
            wbf[("w_out", l_)] = nc.dram_tensor("w_out_bf%d" % l_, [D, D], BF16).ap()
        wbf[("w_up", l_)] = nc.dram_tensor("w_up_bf%d" % l_, [D, 2 * DFF], BF16).ap()
        wbf[("w_dn", l_)] = nc.dram_tensor("w_dn_bf%d" % l_, [DFF, D], BF16).ap()
    wbf_t = {k: Tile("wbf_%s%d" % k) for k in wbf}

    P = Prog(nc)
    ARENA_BYTES = 207 * 1024
    arena_t = es.enter_context(nc.sbuf_tensor("arena", [128, ARENA_BYTES // 2], BF16))
    A = Arena(P, arena_t[:], ARENA_BYTES)
    ps_bufs = []
    for i in range(8):
        pt = es.enter_context(nc.psum_tensor("ps%d" % i, [128, 512], F32))
        ps_bufs.append(Buf(pt[:], Tile("ps%d" % i)))
    PS = Ring(ps_bufs)

    if debug:
        xA_ts = [[Tile("xA%d" % i) for i in range(NT)] for l in range(2)]
        xB_ts = [[Tile("xB%d" % i) for i in range(NT)] for l in range(2)]
        ob_ts = [[Tile("ob%d" % i) for i in range(NT)] for l in range(2)]
    else:
        xA_ts = [[Tile("xA%d" % i) for i in range(NT)]] * 2
        xB_ts = [[Tile("xB%d" % i) for i in range(NT)]] * 2
        ob_ts = [[Tile("ob%d" % i) for i in range(NT)]] * 2
    ada_t = [Tile("ada%d" % l) for l in range(2)]
    out_t = [Tile("out%d" % i) for i in range(NT_LAT)]

    def dma(q, out_ap, in_ap, reads, writes, **kw):
        return P.op(q, lambda e: e.dma_start(out=out_ap, in_=in_ap, **kw), reads, writes, dma=True)

    def mm(out_ap, lhsT, rhs, start, stop, reads, writes, skip=False):
        if skip:
            return P.op("pe", lambda e: e.matmul(out_ap, lhsT, rhs, start=start, stop=stop, skip_group_check=True),
                        reads, writes)
        return P.op("pe", lambda e: e.matmul(out_ap, lhsT, rhs, start=start, stop=stop), reads, writes)

    def act(out_ap, in_ap, func, reads, writes, eng="act", **kw):
        return P.op(eng, lambda e: e.activation(out=out_ap, in_=in_ap, func=func, **kw), reads, writes)

    def tt(eng, out_ap, a, b, op, reads, writes):
        return P.op(eng, lambda e: e.tensor_tensor(out_ap, a, b, op), reads, writes)

    def ts(eng, out_ap, a, s1, s2, op0, op1, reads, writes):
        if op1 is None:
            return P.op(eng, lambda e: e.tensor_scalar(out_ap, a, s1, None, op0), reads, writes)
        return P.op(eng, lambda e: e.tensor_scalar(out_ap, a, s1, s2, op0, op1), reads, writes)

    def stt(eng, out_ap, in0, scalar, in1, op0, op1, reads, writes):
        return P.op(eng, lambda e: e.scalar_tensor_tensor(out=out_ap, in0=in0, scalar=scalar, in1=in1,
                                                          op0=op0, op1=op1), reads, writes)

    def cp(eng, out_ap, in_ap, reads, writes):
        if eng == "act":
            return P.op("act", lambda e: e.copy(out_ap, in_ap), reads, writes)
        return P.op(eng, lambda e: e.tensor_copy(out_ap, in_ap), reads, writes)

    def recip(out_ap, in_ap, reads, writes):
        return P.op("dve", lambda e: e.reciprocal(out_ap, in_ap), reads, writes)

    def rsqrt_cols(col_out, col_in, scale, buf, tmpcol):
        ts("dve", tmpcol, col_in, scale, EPS, ALU.mult, ALU.add, T(buf), T(buf))
        act(tmpcol, tmpcol, AF.Ln, T(buf), T(buf))
        act(col_out, tmpcol, AF.Exp, T(buf), T(buf), scale=-0.5)

    cst = A.alloc("cst", [1024], F32)
    dma("sp", cst[:], consts_d[:, :], [], T(cst))
    ident_f = cst[:, 0:128]
    maskF = cst[:, 128:256]
    maskB = cst[:, 256:384]
    scanmask = cst[:, 384:896]
    rowmask = cst[:, 896:900]
    ident_b = A.alloc("ident_b", [128], BF16)
    cp("dve", ident_b[:], ident_f, T(cst), T(ident_b))

    lbc = A.alloc("lbc", [16], F32)
    scT = A.alloc("scT", [8, 2], F32)

    modc = A.alloc("modc", [2, 4, 8], F32)
    persist_mark = A.mark()

    def prologue():
        m0 = A.mark()
        lbl = A.alloc("lbl", [2, 2, 4], F32)
        for di, src in enumerate((lbf_d, lbb_d)):
            for l in range(2):
                dma("sp", lbl[:, di, l, :], src[l].rearrange("(h k) -> k h", k=128), [], T(lbl),
                    allow_slow_non_contiguous=True)
        for di in range(2):
            o0 = di * 8
            tt("dve", lbc[:, o0:o0 + 4], lbl[:, di, 1, :], lbl[:, di, 0, :], ALU.subtract, T(lbl), T(lbc))
            act(lbc[:, o0:o0 + 4], lbc[:, o0:o0 + 4], AF.Exp, T(lbc), T(lbc), scale=-1.0)
            ts("dve", lbc[:, o0:o0 + 4], lbc[:, o0:o0 + 4], 1.0, None, ALU.add, None, T(lbc), T(lbc))
            recip(lbc[:, o0:o0 + 4], lbc[:, o0:o0 + 4], T(lbc), T(lbc))
            ts("dve", lbc[:, o0 + 4:o0 + 8], lbc[:, o0:o0 + 4], -1.0, 1.0, ALU.mult, ALU.add, T(lbc), T(lbc))
        cct = A.alloc("cct", [8, 2], F32)
        tmp = A.alloc("cctmp", [8, 2], F32)
        dma("sp", cct[:], cc_d[:, :, :], [], T(cct))
        act(tmp[:], cct[:], AF.Exp, T(cct), T(tmp), scale=-1.0)
        ts("dve", tmp[:], tmp[:], 1.0, None, ALU.add, None, T(tmp), T(tmp))
        recip(tmp[:], tmp[:], T(tmp), T(tmp))
        tt("dve", scT[:], cct[:], tmp[:], ALU.mult, T(cct, tmp), T(scT))
        stage = A.ring("adastg", 2, [8, 512], F32)
        adarow = A.alloc("adarow", [6 * D], F32, parts=2)
        brow = A.alloc("brow", [6 * D], F32, parts=2)
        for l in range(n_layers):
            dma("sp", brow[:], b_ada_d[l:l + 1, :].broadcast_to([2, 6 * D]), [], T(brow))
            wv = w_ada_d[l].rearrange("(c p) n -> p c n", p=128)
            for cb in range(12):
                st = stage.get()
                dma("sp", st[:], wv[:, :, cb * 512:(cb + 1) * 512], [], T(st))
                ps = PS.get()
                for c in range(8):
                    mm(ps[0:2, :], scT[:, c, :], st[:, c, :], c == 0, c == 7, T(scT, st), T(ps))
                tt("dve", adarow[:, cb * 512:(cb + 1) * 512], ps[0:2, :], brow[:, cb * 512:(cb + 1) * 512],
                   ALU.add, T(ps, brow), T(adarow))
            dma("sp", ada_d[l], adarow[:], T(adarow), [ada_t[l]])
        A.release(m0)

    def layer_consts(l):
        m0 = A.mark()
        raw = A.alloc("modraw", [2, 4, 8], F32)
        nrm = A.alloc("nrm", [2, 8], F32)
        for m in range(2):
            for k, base in enumerate((0, 1024, 3072, 4096)):
                dma("sp", raw[:, m, k, :], ada_d[l, m, base:base + 1024].rearrange("(c p) -> p c", p=128),
                    [ada_t[l]], T(raw), allow_slow_non_contiguous=True)
        dma("sp", nrm[:, 0, :], norm_mix_d[l].rearrange("(c p) -> p c", p=128), [], T(nrm),
            allow_slow_non_contiguous=True)
        dma("sp", nrm[:, 1, :], norm_ffn_d[l].rearrange("(c p) -> p c", p=128), [], T(nrm),
            allow_slow_non_contiguous=True)
        for m in range(2):
            stt("dve", modc[:, m, 0, :], raw[:, m, 1, :], 1.0, nrm[:, 0, :], ALU.add, ALU.mult, T(raw, nrm), T(modc))
            cp("dve", modc[:, m, 1, :], raw[:, m, 0, :], T(raw), T(modc))
            stt("dve", modc[:, m, 2, :], raw[:, m, 3, :], 1.0, nrm[:, 1, :], ALU.add, ALU.mult, T(raw, nrm), T(modc))
            cp("dve", modc[:, m, 3, :], raw[:, m, 2, :], T(raw), T(modc))
        A.release(m0)

    def src_tile_ap(l, i, stage):
        if stage == "mix":
            if l == 0:
                if i < NT_CTX:
                    return ctx_d[i * 128:(i + 1) * 128, :], []
                return x_d[(i - 2) * 128:(i - 1) * 128, :], []
            return xB_ds[l - 1][i * 128:(i + 1) * 128, :], [xB_ts[l - 1][i]]
        return xA_ds[l][i * 128:(i + 1) * 128, :], [xA_ts[l][i]]

    def norm_transpose(xt, m, which, hT_out_fn, fr):
        st = fr["st"].get()
        xs = fr["xs"].get()
        junk = fr["junk"] if fr.get("junk") is not None else xs
        P.op("pool", lambda e: e.memset(st[:, 0:1], 0.0), [], T(st))
        act(junk[:], xt[:], AF.Square, T(xt), T(junk, st), accum_out=st[:, 0:1])
        rsqrt_cols(st[:, 2:3], st[:, 0:1], 1.0 / D, st, st[:, 1:2])
        act(xs[:], xt[:], AF.Identity, T(xt, st), T(xs), scale=st[:, 2:3])
        gi, si = (0, 1) if which == "mix" else (2, 3)
        for half in range(2):
            ps = fr.get("PS", PS).get()
            for cc in range(4):
                c = half * 4 + cc
                mm(ps[:, cc * 128:(cc + 1) * 128], xs[:, c * 128:(c + 1) * 128], ident_b[:], True, True,
                   T(xs, ident_b), T(ps))
            for cc in range(4):
                c = half * 4 + cc
                eng = "act" if cc % 2 == 0 else "dve"
                if eng == "act":
                    act(hT_out_fn(c), ps[:, cc * 128:(cc + 1) * 128], AF.Identity, T(ps, modc), fr["hT_w"],
                        scale=modc[:, m, gi, c:c + 1], bias=modc[:, m, si, c:c + 1])
                else:
                    ts("dve", hT_out_fn(c), ps[:, cc * 128:(cc + 1) * 128], modc[:, m, gi, c:c + 1],
                       modc[:, m, si, c:c + 1], ALU.mult, ALU.add, T(ps, modc), fr["hT_w"])

    def mixer_phase(l):
        last = (l == n_layers - 1)
        xA_d, xA_t, ob_d, ob_t = xA_ds[l], xA_ts[l], ob_ds[l], ob_ts[l]
        m0 = A.mark()
        w_in = A.alloc("w_in", [8, PROJ], BF16)
        w_out = A.alloc("w_out", [8, D], BF16)
        if l == 0:
            wiv = w_in_d[l].rearrange("(c p) n -> p c n", p=128)
            wov = w_out_d[l].rearrange("(c p) n -> p c n", p=128)
            for c in range(8):
                dma("pool", w_in[:, c, :], wiv[:, c, :], [], T(w_in))
            for c in range(0, 8, 4):
                dma("pool", w_out[:, c:c + 4, :], wov[:, c:c + 4, :], [], T(w_out))
        else:
            wiv = wbf[("w_in", l)].rearrange("(c p) n -> p c n", p=128)
            wov = wbf[("w_out", l)].rearrange("(c p) n -> p c n", p=128)
            for c in range(8):
                dma("sp", w_in[:, c, :], wiv[:, c, :], [wbf_t[("w_in", l)]], T(w_in))
            for c in range(0, 8, 4):
                dma("sp", w_out[:, c:c + 4, :], wov[:, c:c + 4, :], [wbf_t[("w_out", l)]], T(w_out))
        bg_dmas = []
        if l == 0:
            srcs = {"w_in": w_in_d, "w_out": w_out_d, "w_up": w_up_d, "w_dn": w_down_d}
            for key in (("w_up", 0), ("w_dn", 0), ("w_in", 1), ("w_out", 1), ("w_up", 1), ("w_dn", 1)):
                if key[1] >= n_layers:
                    continue
                dst = wbf[key]
                src_full = srcs[key[0]][key[1]]
                rows = dst.shape[0]
                step = 256
                for r0 in range(0, rows, step):
                    bg_dmas.append((dst[r0:r0 + step, :], src_full[r0:r0 + step, :], wbf_t[key]))
        S_f = A.ring("S_f", 2, [512], F32)
        S_b = A.ring("S_b", 2, [512], F32)
        g1_b = [A.alloc("g1b%d" % m, [1024], F32) for m in range(2)]
        for m in range(2):
            dma("sp", g1_b[m][:], ada_d[l, m:m + 1, 2048:3072].broadcast_to([128, 1024]), [ada_t[l]], T(g1_b[m]))
        hgb = A.alloc("hgb", [512], F32)
        lngb = A.alloc("lngb", [512], F32)
        lnbb = A.alloc("lnbb", [512], F32)
        bsb = A.alloc("bsb", [512], F32)
        wsn = A.alloc("wsn", [4, 128], BF16)
        wsT = A.alloc("wsT", [4, 128], BF16)
        dma("sp", hgb[:], hg_norm_d[l:l + 1, :].broadcast_to([128, 512]), [], T(hgb))
        dma("sp", lngb[:], sgu_g_d[l:l + 1, :].broadcast_to([128, 512]), [], T(lngb))
        dma("sp", lnbb[:], sgu_b_d[l:l + 1, :].broadcast_to([128, 512]), [], T(lnbb))
        dma("sp", bsb[:], b_sp_d[l:l + 1, :, :].rearrange("o h p -> o (h p)").broadcast_to([128, 512]), [], T(bsb))
        dma("pool", wsn[:], w_sp_d[l].rearrange("h p q -> p h q"), [], T(wsn))
        ps = PS.get()
        for h in range(4):
            mm(ps[:, h * 128:(h + 1) * 128], wsn[:, h, :], ident_b[:], True, True, T(wsn, ident_b), T(ps))
        cp("act", wsT[:].rearrange("p h q -> p (h q)"), ps[:], T(ps), T(wsT))

        st_r = A.ring("st", 3, [8], F32)
        junk = A.alloc("junk", [1024], BF16)
        xs_r = A.ring("xs", 2, [1024], BF16)
        xring = A.ring("xt", 2, [1024], F32)
        x2ring = A.ring("xt2", 2, [1024], F32)
        hTring = A.ring("hT", 2, [8, 128], BF16)
        tA = A.ring("tA", 6, [512], F32)
        bl_b = A.alloc("bl", [512], F32)
        eb_b = A.alloc("eb", [512], F32)
        KK_r = A.ring("KK", 2, [4, 4, 128], BF16)
        for b in KK_r.bufs:
            P.op("pool", lambda e, b=b: e.memset(b[:], 0.0), [], T(b))
        qdl_r = A.ring("qdl", 2, [512], BF16)
        qdt_r = A.ring("qdt", 3, [512], BF16)
        k2T_r = A.ring("k2T", 2, [512], BF16)
        k2t_r = A.ring("k2t", 2, [512], BF16)
        v_r = A.ring("vbf", 3, [512], BF16)
        scT_r = A.ring("scTb", 2, [512], BF16)
        ecol_r = A.ring("ecol", 3, [48], F32)
        Sbf_r = A.ring("Sbf", 2, [512], BF16)
        obs_r = A.ring("obs", 2, [512], F32)
        gs_r = A.ring("gs", 3, [512], F32)
        gu_r = A.ring("gu", 2, [512], F32)
        vn_r = A.ring("vn", 2, [512], BF16)
        tB = A.ring("tB", 2, [512], F32)
        col_r = A.ring("col", 6, [16], F32)
        ohg_r = A.ring("ohg", 2, [512], BF16)
        oTm_r = A.ring("oTm", 4, [4, 128], BF16)
        oTh_r = A.ring("oTh", 2, [4, 128], BF16)
        xn_r = A.ring("xn", 2, [1024], F32)

        def kk_view(KK, off, n):
            base = KK.ap
            return bass.AP(base.tensor, base.offset + off, [list(base.ap[0]), [128, 4], [544, n], [1, 32]])

        def v4(ap):
            return ap.rearrange("p (h c j) -> p h c j", h=4, c=4)

        def sigmoid_act(dst, src_ap, src_tiles):
            act(dst[:], src_ap, AF.Exp, src_tiles, T(dst), scale=-1.0)
            act(dst[:], dst[:], AF.Ln, T(dst), T(dst), bias=1.0)
            act(dst[:], dst[:], AF.Exp, T(dst), T(dst), scale=-1.0)

        def proj_fm(ps, hT, col0):
            for h in range(4):
                for c in range(8):
                    mm(ps[:, h * 128:(h + 1) * 128], w_in[:, c, col0 + h * 128:col0 + (h + 1) * 128], hT[:, c, :],
                       c == 0, c == 7, T(w_in, hT), T(ps))

        def proj_tm(ps, hT, col0):
            for c in range(8):
                mm(ps[:], hT[:, c, :], w_in[:, c, col0:col0 + 512], c == 0, c == 7, T(w_in, hT), T(ps))

        Sstate = {}

        def tile_gen(i, dirn, Sring):
            is_ctx = i < NT_CTX
            want_o = not (is_ctx and last)
            full = want_o and dirn == "f"
            m = 1 if is_ctx else 0
            di = 0 if dirn == "f" else 1
            edge = 31 if dirn == "f" else 0
            xt = xring.get()
            src, rd = src_tile_ap(l, i, "mix")
            dma("sp", xt[:], src, rd, T(xt))
            st = st_r.get()
            xs = xs_r.get()
            P.op("pool", lambda e: e.memset(st[:, 0:1], 0.0), [], T(st))
            act(junk[:], xt[:], AF.Square, T(xt), T(junk, st), accum_out=st[:, 0:1])
            rsqrt_cols(st[:, 2:3], st[:, 0:1], 1.0 / D, st, st[:, 1:2])
            act(xs[:], xt[:], AF.Identity, T(xt, st), T(xs), scale=st[:, 2:3])
            yield
            hT = hTring.get()
            for half in range(2):
                ps = PS.get()
                for cc in range(4):
                    c = half * 4 + cc
                    mm(ps[:, cc * 128:(cc + 1) * 128], xs[:, c * 128:(c + 1) * 128], ident_b[:], True, True,
                       T(xs, ident_b), T(ps))
                for cc in range(4):
                    c = half * 4 + cc
                    if cc % 2 == 0:
                        act(hT[:, c, :], ps[:, cc * 128:(cc + 1) * 128], AF.Identity, T(ps, modc), T(hT),
                            scale=modc[:, m, 0, c:c + 1], bias=modc[:, m, 1, c:c + 1])
                    else:
                        ts("dve", hT[:, c, :], ps[:, cc * 128:(cc + 1) * 128], modc[:, m, 0, c:c + 1],
                           modc[:, m, 1, c:c + 1], ALU.mult, ALU.add, T(ps, modc), T(hT))
            yield
            psF = PS.get()
            proj_fm(psF, hT, C_FF if dirn == "f" else C_FB)
            psV = PS.get()
            proj_tm(psV, hT, C_I)
            if full:
                psUu = PS.get()
                proj_fm(psUu, hT, C_U)
                psVm = PS.get()
                proj_tm(psVm, hT, C_V)
            if want_o:
                psQ = PS.get()
                proj_fm(psQ, hT, C_Q)
            if full:
                psG = PS.get()
                proj_tm(psG, hT, C_G)
            t1, t2, t3, t4, t5, t6 = tA.get(), tA.get(), tA.get(), tA.get(), tA.get(), tA.get()
            vbf = v_r.get()
            if full:
                gu = gu_r.get()
                act(gu[:], psUu[:], AF.Gelu, T(psUu), T(gu))
                act(t6[:], psVm[:], AF.Gelu, T(psVm), T(t6))
            act(t1[:], psF[:], AF.Exp, T(psF), T(t1), scale=-1.0)
            cp("act", vbf[:], psV[:], T(psV), T(vbf))
            if want_o:
                sigmoid_act(t4, psQ[:], T(psQ))
                tt("dve", t4[:], psQ[:], t4[:], ALU.mult, T(psQ, t4), T(t4))
            if full:
                gs = gs_r.get()
                sigmoid_act(gs, psG[:], T(psG))
                tt("dve", gs[:], psG[:], gs[:], ALU.mult, T(psG, gs), T(gs))
            act(t1[:], t1[:], AF.Ln, T(t1), T(t1), bias=1.0)
            act(t2[:], t1[:], AF.Exp, T(t1), T(t2), scale=-1.0)
            if l > 0:
                t23 = t2[:].rearrange("p (h j) -> p h j", h=4)
                for h in range(4):
                    ts("dve", t23[:, h, :], t23[:, h, :], lbc[:, di * 8 + 4 + h:di * 8 + 5 + h],
                       lbc[:, di * 8 + h:di * 8 + h + 1], ALU.mult, ALU.add, T(t2, lbc), T(t2))
                act(t1[:], t2[:], AF.Ln, T(t2), T(t1))
            act(t2[:], t2[:], AF.Identity, T(t2), T(t2), scale=-1.0, bias=1.0)
            sop = ALU.subtract if l == 0 else ALU.add
            P.op("dve", lambda e: e.tensor_tensor_scan(bl_b[:], scanmask, t1[:], 0.0, ALU.mult, sop),
                 T(cst, t1), T(bl_b))
            Bsrc = bl_b
            if dirn == "b":
                tt("dve", t1[:], bl_b[:], t1[:], ALU.add if l == 0 else ALU.subtract, T(bl_b, t1), T(t1))
                tt("dve", v4(t3[:]), v4(bl_b[:])[:, :, :, 31:32].broadcast_to([128, 4, 4, 32]), v4(t1[:]),
                   ALU.subtract, T(bl_b, t1), T(t3))
                Bsrc = t3
            eb = eb_b
            act(eb[:], Bsrc[:], AF.Exp, T(Bsrc), T(eb))
            act(t1[:], Bsrc[:], AF.Exp, T(Bsrc), T(t1), scale=-1.0)
            eb4 = v4(eb[:])
            tt("dve", t1[:], t2[:], t1[:], ALU.mult, T(t2, t1), T(t1))
            tt("dve", v4(t5[:]), v4(t1[:]), eb4[:, :, :, edge:edge + 1].broadcast_to([128, 4, 4, 32]), ALU.mult,
               T(t1, eb), T(t5))
            ecol = ecol_r.get()
            Plo = ecol[:, 0:16].rearrange("p (h c) -> p h c", c=4)
            Phi = ecol[:, 16:32].rearrange("p (h c) -> p h c", c=4)
            Etile = ecol[:, 32:36]
            e12 = ecol[:, 36:40]
            ec = [eb4[:, :, c, edge] for c in range(4)]
            P.op("dve", lambda e: e.memset(ecol[:, 0:32], 1.0), [], T(ecol))
            cp("dve", Plo[:, :, 1], ec[0], T(eb), T(ecol))
            tt("dve", Plo[:, :, 2], Plo[:, :, 1], ec[1], ALU.mult, T(eb, ecol), T(ecol))
            tt("dve", Plo[:, :, 3], Plo[:, :, 2], ec[2], ALU.mult, T(eb, ecol), T(ecol))
            cp("dve", Phi[:, :, 2], ec[3], T(eb), T(ecol))
            tt("dve", Phi[:, :, 1], Phi[:, :, 2], ec[2], ALU.mult, T(eb, ecol), T(ecol))
            tt("dve", Phi[:, :, 0], Phi[:, :, 1], ec[1], ALU.mult, T(eb, ecol), T(ecol))
            tt("dve", Etile, Plo[:, :, 3], ec[3], ALU.mult, T(eb, ecol), T(ecol))
            tt("dve", e12, ec[1], ec[2], ALU.mult, T(eb), T(ecol))
            Ecum, Esuf = (Plo, Phi) if dirn == "f" else (Phi, Plo)
            k2T = k2T_r.get()
            tt("dve", v4(k2T[:]), v4(t5[:]), Esuf.rearrange("p h (c o) -> p h c o", o=1).broadcast_to([128, 4, 4, 32]),
               ALU.mult, T(t5, ecol), T(k2T))
            if want_o:
                KK = KK_r.get()
                k5 = v4(t5[:])
                e_mid = eb4[:, :, 1:3, edge:edge + 1].broadcast_to([128, 4, 2, 32])
                e12b = e12.rearrange("p (h a b) -> p h a b", a=1, b=1).broadcast_to([128, 4, 1, 32])
                cp("pool", kk_view(KK, 0, 4), v4(t1[:]), T(t1), T(KK))
                if dirn == "f":
                    cp("pool", kk_view(KK, 512, 3), k5[:, :, 0:3, :], T(t5), T(KK))
                    tt("pool", kk_view(KK, 1024, 2), k5[:, :, 0:2, :], e_mid, ALU.mult, T(t5, eb), T(KK))
                    tt("pool", kk_view(KK, 1536, 1), k5[:, :, 0:1, :], e12b, ALU.mult, T(t5, ecol), T(KK))
                else:
                    cp("pool", kk_view(KK, 32, 3), k5[:, :, 1:4, :], T(t5), T(KK))
                    tt("pool", kk_view(KK, 64, 2), k5[:, :, 2:4, :], e_mid, ALU.mult, T(t5, eb), T(KK))
                    tt("pool", kk_view(KK, 96, 1), k5[:, :, 3:4, :], e12b, ALU.mult, T(t5, ecol), T(KK))
                stt("dve", t4[:], t4[:], float(128 ** -0.5), eb[:], ALU.mult, ALU.mult, T(t4, eb), T(t4))
                qdl = qdl_r.get()
                cp("dve", qdl[:], t4[:], T(t4), T(qdl))
                qdt = qdt_r.get()
                tt("dve", v4(qdt[:]), v4(t4[:]), Ecum.rearrange("p h (c o) -> p h c o", o=1).broadcast_to([128, 4, 4, 32]),
                   ALU.mult, T(t4, ecol), T(qdt))
            if full:
                gv3 = t6[:].rearrange("p (h v) -> p h v", h=4)
                col2 = col_r.get()
                P.op("dve", lambda e: e.tensor_reduce(col2[:, 0:4], t6[:].rearrange("p (h v) -> p h v", h=4),
                                                      AX.X, ALU.add), T(t6), T(col2))
                ts("dve", col2[:, 0:4], col2[:, 0:4], 1.0 / 128, None, ALU.mult, None, T(col2), T(col2))
                tt("dve", gv3, gv3, col2[:, 0:4].rearrange("p (h o) -> p h o", o=1).broadcast_to([128, 4, 128]),
                   ALU.subtract, T(t6, col2), T(t6))
                act(t2[:], t6[:], AF.Square, T(t6), T(t2))
                P.op("dve", lambda e: e.tensor_reduce(col2[:, 4:8], t2[:].rearrange("p (h v) -> p h v", h=4),
                                                      AX.X, ALU.add), T(t2), T(col2))
                rsqrt_cols(col2[:, 12:16], col2[:, 4:8], 1.0 / 128, col2, col2[:, 8:12])
                tt("dve", gv3, gv3, col2[:, 12:16].rearrange("p (h o) -> p h o", o=1).broadcast_to([128, 4, 128]),
                   ALU.mult, T(t6, col2), T(t6))
                tt("pool", t6[:], t6[:], lngb[:], ALU.mult, T(t6, lngb), T(t6))
                vn = vn_r.get()
                tt("pool", vn[:], t6[:], lnbb[:], ALU.add, T(t6, lnbb), T(vn))
            yield
            psK = PS.get()
            for h in range(4):
                mm(psK[:, h * 128:(h + 1) * 128], k2T[:, h * 128:(h + 1) * 128], ident_b[:], True, True,
                   T(k2T, ident_b), T(psK))
            k2t = k2t_r.get()
            cp("act", k2t[:], psK[:], T(psK), T(k2t))
            if want_o:
                psS = PS.get()
                for ct in range(4):
                    for h in range(4):
                        mm(psS[:, h * 128 + ct * 32:h * 128 + ct * 32 + 32], KK[:, ct, h, :],
                           qdl[:, h * 128 + ct * 32:h * 128 + ct * 32 + 32], True, True, T(KK, qdl), T(psS))
                scTb = scT_r.get()
                mk = maskF if dirn == "f" else maskB
                tt("dve", scTb[:].rearrange("p (h t) -> p h t", h=4), psS[:].rearrange("p (h t) -> p h t", h=4),
                   mk.rearrange("p (o t) -> p o t", o=1).broadcast_to([128, 4, 128]), ALU.mult, T(psS, cst), T(scTb))
            if full:
                psZ = PS.get()
                for h in range(4):
                    mm(psZ[:, h * 128:(h + 1) * 128], vn[:, h * 128:(h + 1) * 128], wsT[:, h, :], True, True,
                       T(vn, wsT), T(psZ))
                oTm = oTm_r.get()
                tz = tB.get()
                tt("dve", tz[:], psZ[:], bsb[:], ALU.add, T(psZ, bsb), T(tz))
                tt("dve", oTm[:].rearrange("p h t -> p (h t)"), tz[:], gu[:], ALU.mult, T(tz, gu), T(oTm))
                obs = obs_r.get()
                dma("sp", obs[:], ob_d[i * 128:(i + 1) * 128, :], [ob_t[i]], T(obs))
            yield
            S_old = Sring.bufs[Sstate[dirn] % 2]
            S_new = Sring.bufs[(Sstate[dirn] + 1) % 2]
            Sstate[dirn] += 1
            if want_o:
                Sbf = Sstate[dirn + "bf"]
                psO = PS.get()
                for h in range(4):
                    mm(psO[:, h * 128:(h + 1) * 128], scTb[:, h * 128:(h + 1) * 128], vbf[:, h * 128:(h + 1) * 128],
                       h == 0, False, T(scTb, vbf), T(psO), skip=True)
                for h in range(4):
                    mm(psO[:, h * 128:(h + 1) * 128], qdt[:, h * 128:(h + 1) * 128], Sbf[:, h * 128:(h + 1) * 128],
                       False, True, T(qdt, Sbf), T(psO), skip=True)
            psU = PS.get()
            for h in range(4):
                mm(psU[:, h * 128:(h + 1) * 128], k2t[:, h * 128:(h + 1) * 128], vbf[:, h * 128:(h + 1) * 128],
                   True, True, T(k2t, vbf), T(psU))
            for h in range(4):
                stt("dve", S_new[:, h * 128:(h + 1) * 128], S_old[:, h * 128:(h + 1) * 128], Etile[:, h:h + 1],
                    psU[:, h * 128:(h + 1) * 128], ALU.mult, ALU.add, T(S_old, ecol, psU), T(S_new))
            Sbf_n = Sbf_r.get()
            cp("act", Sbf_n[:], S_new[:], T(S_new), T(Sbf_n))
            Sstate[dirn + "bf"] = Sbf_n
            if not want_o:
                return
            if dirn == "b":
                obs = obs_r.get()
                cp("act", obs[:], psO[:], T(psO), T(obs))
                dma("pool", ob_d[i * 128:(i + 1) * 128, :], obs[:], T(obs), [ob_t[i]])
                return
            osum, osq = tB.get(), tB.get()
            col = col_r.get()
            tt("dve", osum[:], psO[:], obs[:], ALU.add, T(psO, obs), T(osum))
            act(osq[:], osum[:], AF.Square, T(osum), T(osq))
            P.op("dve", lambda e: e.tensor_reduce(col[:, 0:4], osq[:].rearrange("p (h v) -> p h v", h=4),
                                                  AX.X, ALU.add), T(osq), T(col))
            rsqrt_cols(col[:, 8:12], col[:, 0:4], 1.0 / 128, col, col[:, 4:8])
            o3 = osum[:].rearrange("p (h v) -> p h v", h=4)
            tt("dve", o3, o3, col[:, 8:12].rearrange("p (h o) -> p h o", o=1).broadcast_to([128, 4, 128]), ALU.mult,
               T(osum, col), T(osum))
            tt("pool", osum[:], osum[:], hgb[:], ALU.mult, T(osum, hgb), T(osum))
            ohg = ohg_r.get()
            tt("dve", ohg[:], osum[:], gs[:], ALU.mult, T(osum, gs), T(ohg))
            yield
            psT2 = PS.get()
            for h in range(4):
                mm(psT2[:, h * 128:(h + 1) * 128], ohg[:, h * 128:(h + 1) * 128], ident_b[:], True, True,
                   T(ohg, ident_b), T(psT2))
            oTh = oTh_r.get()
            cp("act", oTh[:].rearrange("p h t -> p (h t)"), psT2[:], T(psT2), T(oTh))
            xt2 = x2ring.get()
            dma("sp", xt2[:], src, rd, T(xt2))
            yield
            xn = xn_r.get()
            for n in range(2):
                psW = PS.get()
                for c in range(8):
                    lhs = oTh[:, c, :] if c < 4 else oTm[:, c - 4, :]
                    mm(psW[:], lhs, w_out[:, c, n * 512:(n + 1) * 512], c == 0, c == 7, T(oTh, oTm, w_out), T(psW))
                tt("dve", xn[:, n * 512:(n + 1) * 512], psW[:], g1_b[m][:, n * 512:(n + 1) * 512], ALU.mult,
                   T(psW, g1_b[m]), T(xn))
            tt("pool", xn[:], xn[:], xt2[:], ALU.add, T(xn, xt2), T(xn))
            dma("pool", xA_d[i * 128:(i + 1) * 128, :], xn[:], T(xn), [xA_t[i]])

        def run_pass(order, dirn, Sring):
            P.op("pool", lambda e: e.memset(Sring.bufs[0][:], 0.0), [], T(Sring.bufs[0]))
            Sstate[dirn] = 0
            Sbf0 = Sbf_r.get()
            P.op("pool", lambda e: e.memset(Sbf0[:], 0.0), [], T(Sbf0))
            Sstate[dirn + "bf"] = Sbf0
            live = []
            pending = list(order)
            while pending or live:
                for g in list(live):
                    try:
                        next(g)
                    except StopIteration:
                        live.remove(g)
                if pending:
                    g = tile_gen(pending.pop(0), dirn, Sring)
                    next(g)
                    live.append(g)
                if bg_dmas and len(live) >= 3:
                    o_ap, i_ap, wt = bg_dmas.pop(0)
                    dma("pool", o_ap, i_ap, [], [wt])

        run_pass([1, 0] + list(range(NT - 1, NT_CTX - 1, -1)), "b", S_b)
        run_pass(list(range(NT)), "f", S_f)
        while bg_dmas:
            o_ap, i_ap, wt = bg_dmas.pop(0)
            dma("pool", o_ap, i_ap, [], [wt])
        A.release(m0)


    def ffn_phase(l):
        last = (l == n_layers - 1)
        xA_d, xA_t, xB_d, xB_t = xA_ds[l], xA_ts[l], xB_ds[l], xB_ts[l]
        m0 = A.mark()
        w_up = A.alloc("w_up", [8, 2 * DFF], BF16)
        w_dn = A.alloc("w_dn", [NFC, D], BF16)
        wuv = wbf[("w_up", l)].rearrange("(c p) n -> p c n", p=128)
        wdv = wbf[("w_dn", l)].rearrange("(c p) n -> p c n", p=128)
        for c in range(8):
            for hlf in range(2):
                dma("sp", w_up[:, c, hlf * DFF:(hlf + 1) * DFF], wuv[:, c, hlf * DFF:(hlf + 1) * DFF],
                    [wbf_t[("w_up", l)]], T(w_up))
        for c in range(0, NFC, 2):
            dma("sp", w_dn[:, c:c + 2, :], wdv[:, c:c + 2, :], [wbf_t[("w_dn", l)]], T(w_dn))
        cw = A.alloc("cw", [NFC, 9], F32)
        cb = A.alloc("cb", [NFC], F32)
        for t9 in range(9):
            dma("sp", cw[:, :, t9], conv_w_d[l, t9 // 3, t9 % 3, :].rearrange("(c p) -> p c", p=128), [], T(cw),
                allow_slow_non_contiguous=True)
        dma("sp", cb[:], conv_b_d[l].rearrange("(c p) -> p c", p=128), [], T(cb), allow_slow_non_contiguous=True)
        g2_b = [A.alloc("g2b%d" % m, [1024], F32) for m in range(2)]
        for m in range(2):
            dma("sp", g2_b[m][:], ada_d[l, m:m + 1, 5120:6144].broadcast_to([128, 1024]), [ada_t[l]], T(g2_b[m]))
        nfb = None
        if last:
            nfb = A.alloc("nfb", [1024], F32)
            dma("sp", nfb[:], norm_final_d.rearrange("(o n) -> o n", o=1).broadcast_to([128, 1024]), [], T(nfb))

        fr = {
            "st": A.ring("fst", 2, [8], F32),
            "junk": None,
            "xs": A.ring("fxs", 1, [1024], BF16),
        }
        xring = A.ring("fxt", 2, [1024], F32)
        h2T = A.alloc("h2T", [8, 640], BF16)
        mT = A.alloc("mT", [NFC, 512], BF16)
        Gsb_r = A.ring("Gsb", 2, [640], F32)
        acc_r = A.ring("acc", 2, [512], F32)
        xs0 = fr["xs"].bufs[0]
        asb_r = Ring([A.alloc("asb0", [512], F32), Buf(xs0.ap.bitcast(F32), xs0.t)])
        xn_r = A.ring("fxn", 1, [1024], F32)
        fcol_r = A.ring("fcol", 2, [8], F32)

        def block(kind, b):
            if kind == "lat":
                m = 0
                tiles = [2 + 4 * b + k for k in range(4)]
                ntok = 512
                W, nrows, ro = 64, 8, 1
                has_prev, has_next = b > 0, b < 7
                taps = [(ky, kx) for ky in range(3) for kx in range(3)]
            else:
                m = 1
                tiles = [0, 1]
                ntok = 256
                W, nrows, ro = 256, 1, 0
                has_prev = has_next = False
                taps = [(1, kx) for kx in range(3)]
            halo = kind == "lat"
            fr["hT_w"] = T(h2T)
            for k, i in enumerate(tiles):
                xt = xring.get()
                src, rd = src_tile_ap(l, i, "ffn")
                dma("sp", xt[:], src, rd, T(xt))
                norm_transpose(xt, m, "ffn", lambda c, k=k: h2T[:, c, k * 128:(k + 1) * 128], fr)
            if halo:
                xt = xring.get()
                t_first, t_last = tiles[0], tiles[-1]
                ip = t_first - 1 if has_prev else t_first
                inx = t_last + 1 if has_next else t_last
                dma("sp", xt[0:64, :], xA_d[ip * 128 + 64:ip * 128 + 128, :], [xA_t[ip]], T(xt))
                dma("sp", xt[64:128, :], xA_d[inx * 128:inx * 128 + 64, :], [xA_t[inx]], T(xt))
                norm_transpose(xt, m, "ffn", lambda c: h2T[:, c, 512:640], fr)
            pending = None
            for fc in range(NFC):
                psA = PS.get()
                for c in range(8):
                    mm(psA[:, 0:ntok], w_up[:, c, fc * 128:(fc + 1) * 128], h2T[:, c, 0:ntok], c == 0, c == 7,
                       T(w_up, h2T), T(psA))
                asb = asb_r.get()
                cp("act", asb[:, 0:ntok], psA[:, 0:ntok], T(psA), T(asb))
                psG = PS.get()
                for c in range(8):
                    mm(psG[:, 0:ntok], w_up[:, c, DFF + fc * 128:DFF + (fc + 1) * 128], h2T[:, c, 0:ntok], c == 0, c == 7,
                       T(w_up, h2T), T(psG))
                Gsb = Gsb_r.get()
                if halo:
                    psH = PS.get()
                    for c in range(8):
                        mm(psH[:, 0:128], w_up[:, c, DFF + fc * 128:DFF + (fc + 1) * 128], h2T[:, c, 512:640], c == 0,
                           c == 7, T(w_up, h2T), T(psH))
                    G3 = Gsb[:].rearrange("p (r w) -> p r w", w=64)
                    cp("act", Gsb[:, 64:576], psG[:, 0:512], T(psG), T(Gsb))
                    if has_prev:
                        cp("act", Gsb[:, 0:64], psH[:, 0:64], T(psH), T(Gsb))
                    else:
                        P.op("pool", lambda e, Gsb=Gsb: e.memset(Gsb[:, 0:64], 0.0), [], T(Gsb))
                    if has_next:
                        cp("act", Gsb[:, 576:640], psH[:, 64:128], T(psH), T(Gsb))
                    else:
                        P.op("pool", lambda e, Gsb=Gsb: e.memset(Gsb[:, 576:640], 0.0), [], T(Gsb))
                else:
                    G3 = Gsb[:, 0:ntok].rearrange("p (r w) -> p r w", w=W)
                    cp("act", Gsb[:, 0:ntok], psG[:, 0:ntok], T(psG), T(Gsb))
                acc = acc_r.get()
                a3 = acc[:, 0:ntok].rearrange("p (r w) -> p r w", w=W)
                act(acc[:, 0:ntok].rearrange("p (r w) -> p r w", w=W), G3[:, ro:ro + nrows, :], AF.Identity,
                    T(Gsb, cw, cb), T(acc), scale=cw[:, fc, 4:5], bias=cb[:, fc:fc + 1])
                if pending is not None:
                    act(pending[2][:, 0:ntok], pending[2][:, 0:ntok], AF.Gelu, T(pending[2]), T(pending[2]))
                    pf, pA, pacc = pending
                    tt("dve", mT[:, pf, 0:ntok], pA[:, 0:ntok], pacc[:, 0:ntok], ALU.mult, T(pA, pacc), T(mT))
                    pending = None
                for (ky, kx) in taps:
                    if ky == 1 and kx == 1:
                        continue
                    r_src = ro + ky - 1
                    if kx == 0:
                        o_ap, i_ap = a3[:, :, 1:W], G3[:, r_src:r_src + nrows, 0:W - 1]
                    elif kx == 1:
                        o_ap, i_ap = a3[:, :, :], G3[:, r_src:r_src + nrows, :]
                    else:
                        o_ap, i_ap = a3[:, :, 0:W - 1], G3[:, r_src:r_src + nrows, 1:W]
                    t9 = ky * 3 + kx
                    stt("dve", o_ap, i_ap, cw[:, fc, t9:t9 + 1], o_ap, ALU.mult, ALU.add, T(Gsb, cw, acc), T(acc))
                if pending is not None:
                    pf, pA, pacc = pending
                    tt("dve", mT[:, pf, 0:ntok], pA[:, 0:ntok], pacc[:, 0:ntok], ALU.mult, T(pA, pacc), T(mT))
                pending = (fc, asb, acc)
            pf, pA, pacc = pending
            act(pacc[:, 0:ntok], pacc[:, 0:ntok], AF.Gelu, T(pacc), T(pacc))
            tt("dve", mT[:, pf, 0:ntok], pA[:, 0:ntok], pacc[:, 0:ntok], ALU.mult, T(pA, pacc), T(mT))
            for k, i in enumerate(tiles):
                xt = xring.get()
                src, rd = src_tile_ap(l, i, "ffn")
                dma("sp", xt[:], src, rd, T(xt))
                xn = xn_r.get()
                for n in range(2):
                    psW = PS.get()
                    for fc in range(NFC):
                        mm(psW[:], mT[:, fc, k * 128:(k + 1) * 128], w_dn[:, fc, n * 512:(n + 1) * 512], fc == 0,
                           fc == NFC - 1, T(mT, w_dn), T(psW))
                    tt("dve", xn[:, n * 512:(n + 1) * 512], psW[:], g2_b[m][:, n * 512:(n + 1) * 512], ALU.mult,
                       T(psW, g2_b[m]), T(xn))
                tt("pool", xn[:], xn[:], xt[:], ALU.add, T(xn, xt), T(xn))
                if last:
                    col = fcol_r.get()
                    P.op("pool", lambda e, col=col: e.memset(col[:, 0:1], 0.0), [], T(col))
                    jk = fr["xs"].bufs[0]
                    act(jk[:], xn[:], AF.Square, T(xn), T(jk, col), accum_out=col[:, 0:1])
                    rsqrt_cols(col[:, 2:3], col[:, 0:1], 1.0 / D, col, col[:, 1:2])
                    stt("dve", xn[:], xn[:], col[:, 2:3], nfb[:], ALU.mult, ALU.mult, T(xn, col, nfb), T(xn))
                    j = i - 2
                    dma("pool", out_d[j * 128:(j + 1) * 128, :], xn[:], T(xn), [out_t[j]])
                else:
                    dma("pool", xB_d[i * 128:(i + 1) * 128, :], xn[:], T(xn), [xB_t[i]])

        if not last:
            block("ctx", 0)
        for b in range(8):
            block("lat", b)
        A.release(m0)

    prologue()
    for l in range(n_layers):
        layer_consts(l)
        mixer_phase(l)
        ffn_phase(l)
    stats = P.emit()
    es.close()
    return nc, stats, A.peak


_CACHE = {}


def kernel(_debug=False, **inputs):
    key = "nc_dbg" if _debug else "nc"
    if key not in _CACHE:
        _CACHE[key] = build_program(debug=_debug)[0]
    nc = _CACHE[key]
    f = lambda a: np.ascontiguousarray(np.asarray(a, dtype=np.float32))
    x = f(inputs["x"])
    c = f(inputs["c"])
    ctx = f(inputs["ctx"])
    c_ctx = f(inputs["c_ctx"])
    consts = host_consts()
    shared = {k: f(inputs[k]) for k in (
        "w_ada", "b_ada", "norm_mix", "norm_ffn", "w_in", "lb_logits_fwd", "lb_logits_bwd", "hg_norm",
        "sgu_norm_g", "sgu_norm_b", "w_spatial", "b_spatial", "w_out", "w_up", "conv_w", "conv_b", "w_down",
        "norm_final")}
    in_maps = []
    for b in range(8):
        cc = np.stack([c[b], c_ctx], axis=-1).reshape(8, 128, 2).transpose(1, 0, 2)
        d = {"x": x[b], "ctx": ctx[b], "cc": np.ascontiguousarray(cc), "consts": consts}
        d.update(shared)
        in_maps.append(d)
    res = run_bass_kernel_spmd(nc, in_maps, core_ids=list(range(8)))
    if _debug:
        return res.results
    return np.stack([np.asarray(res.results[b]["out"], dtype=np.float32) for b in range(8)], axis=0)


if __name__ == "__main__":
    import time
    t0 = time.time()
    nc, stats, peak = build_program()
    print("build", time.time() - t0, stats, "arena peak", peak)
```

```python
import numpy as np
from collections import deque
from contextlib import ExitStack

import concourse.bass as bass
import concourse.mybir as mybir
from concourse.bass_utils import run_bass_kernel_spmd

F32 = mybir.dt.float32
BF16 = mybir.dt.bfloat16
AF = mybir.ActivationFunctionType
ALU = mybir.AluOpType
AX = mybir.AxisListType

COMPUTE = ("pe", "act", "dve", "pool")
ENGS = ("pe", "act", "dve", "pool", "sp")

D = 1024
T_LAT = 4096
T_CTX = 256
NT_LAT = 32
NT_CTX = 2
NT = 34
HW = 512
PROJ = 3584
DFF = 2816
NFC = 22
EPS = 1e-6
C_FF, C_FB, C_I, C_Q, C_G, C_U, C_V = 0, 512, 1024, 1536, 2048, 2560, 3072


class Tile:
    __slots__ = ("name", "writers", "readers")

    def __init__(self, name, init=None):
        self.name = name
        self.writers = list(init) if init else []
        self.readers = []


class Op:
    __slots__ = ("eng", "fn", "deps", "inc", "val", "dma", "sem", "prev_dma")

    def __init__(self, eng, fn, dma):
        self.eng = eng
        self.fn = fn
        self.deps = []
        self.inc = False
        self.val = 0
        self.dma = dma
        self.sem = None
        self.prev_dma = None


class Prog:
    def __init__(self, nc, n_dma_sems=16):
        self.nc = nc
        self.ops = []
        self.n_dma_sems = n_dma_sems
        self.last_op = {}
        self.recent_dma = {e: deque(maxlen=n_dma_sems) for e in ENGS}

    def frontier(self):
        f = [o for o in self.last_op.values()]
        for e in ENGS:
            f.extend(self.recent_dma[e])
        return f

    def op(self, eng, fn, reads=(), writes=(), dma=False):
        o = Op(eng, fn, dma)
        deps = {}
        for t in reads:
            for w in t.writers:
                if (not dma) and (not w.dma) and w.eng == eng and eng == "pe":
                    continue
                deps[id(w)] = w
        for t in writes:
            for w in t.writers:
                if (not dma) and (not w.dma) and w.eng == eng:
                    continue
                deps[id(w)] = w
            for r in t.readers:
                if (not dma) and (not r.dma) and r.eng == eng:
                    continue
                deps[id(r)] = r
        for t in reads:
            t.readers.append(o)
        for t in writes:
            t.writers = [o]
            t.readers = []
        o.deps = list(deps.values())
        for d in o.deps:
            d.inc = True
        self.ops.append(o)
        if dma:
            self.recent_dma[eng].append(o)
        else:
            self.last_op[eng] = o
        return o

    def emit(self):
        nc = self.nc
        by_eng = {e: [] for e in ENGS}
        for o in self.ops:
            by_eng[o.eng].append(o)
        stats = {}
        with ExitStack() as es:
            esem = {e: es.enter_context(nc.semaphore("s_" + e)) for e in COMPUTE}
            dsem = {}
            for e in ENGS:
                if any(o.dma for o in by_eng[e]):
                    dsem[e] = [es.enter_context(nc.semaphore("d_%s_%d" % (e, i)))
                               for i in range(self.n_dma_sems)]
            for e in ENGS:
                cnt = 0
                dcnt = [0] * self.n_dma_sems
                last_on_slot = [None] * self.n_dma_sems
                k = 0
                for o in by_eng[e]:
                    if o.dma:
                        j = k % self.n_dma_sems
                        k += 1
                        dcnt[j] += 16
                        o.sem = dsem[e][j]
                        o.val = dcnt[j]
                        o.prev_dma = last_on_slot[j]
                        last_on_slot[j] = o
                    else:
                        if o.inc:
                            cnt += 1
                        o.sem = esem[e]
                        o.val = cnt
            block = es.enter_context(nc.Block())

            def run_engine(e, engobj):
                known = {}
                nwait = [0]

                def wait(sem, val):
                    if val <= 0:
                        return
                    key = sem.num
                    if known.get(key, 0) >= val:
                        return
                    engobj.wait_ge(sem, val)
                    known[key] = val
                    nwait[0] += 1

                for o in by_eng[e]:
                    for d in o.deps:
                        wait(d.sem, d.val)
                    if o.dma and o.prev_dma is not None:
                        wait(o.prev_dma.sem, o.prev_dma.val)
                    ins = o.fn(engobj)
                    if o.dma:
                        ins.then_inc(o.sem, 16)
                    elif o.inc:
                        ins.then_inc(o.sem, 1)
                if e in dsem:
                    last = {}
                    for o in by_eng[e]:
                        if o.dma:
                            last[o.sem.num] = o
                    for o in last.values():
                        wait(o.sem, o.val)
                stats[e] = (len(by_eng[e]), nwait[0])

            @block.sync
            def _(eng):
                run_engine("sp", eng)

            @block.tensor
            def _(eng):
                run_engine("pe", eng)

            @block.scalar
            def _(eng):
                run_engine("act", eng)

            @block.vector
            def _(eng):
                run_engine("dve", eng)

            @block.gpsimd
            def _(eng):
                run_engine("pool", eng)
        return stats


class Buf:
    __slots__ = ("ap", "t")

    def __init__(self, ap, t):
        self.ap = ap
        self.t = t

    def __getitem__(self, idx):
        return self.ap[idx]


class Ring:
    def __init__(self, bufs):
        self.bufs = bufs
        self.i = 0

    def get(self):
        b = self.bufs[self.i % len(self.bufs)]
        self.i += 1
        return b


class Arena:
    def __init__(self, P, arena_ap, nbytes):
        self.P = P
        self.a = arena_ap
        self.n = nbytes
        self.off = 0
        self.reused = False
        self.peak = 0

    def mark(self):
        return self.off

    def release(self, m):
        self.off = m
        self.reused = True

    def alloc(self, name, free_shape, dtype, parts=128):
        esz = 4 if dtype == F32 else 2
        n = 1
        for s in free_shape:
            n *= s
        nb = (n * esz + 63) // 64 * 64
        assert self.off + nb <= self.n, "arena overflow at %s: %d + %d > %d" % (name, self.off, nb, self.n)
        ap = self.a[0:parts, self.off // 2:self.off // 2 + (n * esz) // 2]
        if dtype == F32:
            ap = ap.bitcast(F32)
        if len(free_shape) == 2:
            ap = ap.rearrange("p (a b) -> p a b", b=free_shape[1])
        elif len(free_shape) == 3:
            ap = ap.rearrange("p (a b c) -> p a b c", b=free_shape[1], c=free_shape[2])
        self.off += nb
        self.peak = max(self.peak, self.off)
        t = Tile(name, self.P.frontier() if self.reused else None)
        return Buf(ap, t)

    def ring(self, name, n, free_shape, dtype):
        return Ring([self.alloc("%s%d" % (name, i), free_shape, dtype) for i in range(n)])


def T(*bufs):
    return [b.t for b in bufs]


def host_consts():
    c = np.zeros((128, 1024), np.float32)
    c[:, 0:128] = np.eye(128, dtype=np.float32)
    s = np.arange(128)[:, None]
    t = np.arange(128)[None, :]
    same = (s // 32) == (t // 32)
    c[:, 128:256] = (s <= t).astype(np.float32)
    c[:, 256:384] = (s >= t).astype(np.float32)
    j = np.arange(512)
    c[:, 384:896] = (j % 32 != 0).astype(np.float32)[None, :]
    for cc in range(4):
        c[:, 896 + cc] = ((np.arange(128) // 32) == cc).astype(np.float32)
    c[:, 900] = 1.0
    return c


def build_program(n_layers=2, debug=False):
    nc = bass.Bass("TRN2", target_bir_lowering=False)
    es = ExitStack()

    def din(name, shape):
        return nc.dram_tensor(name, list(shape), F32, kind="ExternalInput").ap()

    x_d = din("x", [T_LAT, D])
    ctx_d = din("ctx", [T_CTX, D])
    cc_d = din("cc", [128, 8, 2])
    consts_d = din("consts", [128, 1024])
    w_ada_d = din("w_ada", [2, D, 6 * D])
    b_ada_d = din("b_ada", [2, 6 * D])
    norm_mix_d = din("norm_mix", [2, D])
    norm_ffn_d = din("norm_ffn", [2, D])
    w_in_d = din("w_in", [2, D, PROJ])
    lbf_d = din("lb_logits_fwd", [2, HW])
    lbb_d = din("lb_logits_bwd", [2, HW])
    hg_norm_d = din("hg_norm", [2, HW])
    sgu_g_d = din("sgu_norm_g", [2, HW])
    sgu_b_d = din("sgu_norm_b", [2, HW])
    w_sp_d = din("w_spatial", [2, 4, 128, 128])
    b_sp_d = din("b_spatial", [2, 4, 128])
    w_out_d = din("w_out", [2, D, D])
    w_up_d = din("w_up", [2, D, 2 * DFF])
    conv_w_d = din("conv_w", [2, 3, 3, DFF])
    conv_b_d = din("conv_b", [2, DFF])
    w_down_d = din("w_down", [2, DFF, D])
    norm_final_d = din("norm_final", [D])
    out_d = nc.dram_tensor("out", [T_LAT, D], F32, kind="ExternalOutput").ap()

    if debug:
        kw = dict(kind="ExternalOutput")
        xA_ds = [nc.dram_tensor("xA%d" % l, [NT * 128, D], F32, **kw).ap() for l in range(2)]
        xB_ds = [nc.dram_tensor("xB%d" % l, [NT * 128, D], F32, **kw).ap() for l in range(2)]
        ob_ds = [nc.dram_tensor("obs%d" % l, [NT * 128, HW], F32, **kw).ap() for l in range(2)]
        ada_d = nc.dram_tensor("ada_s", [2, 2, 6 * D], F32, **kw).ap()
    else:
        xA_ds = [nc.dram_tensor("xA", [NT * 128, D], F32).ap()] * 2
        xB_ds = [nc.dram_tensor("xB", [NT * 128, D], F32).ap()] * 2
        ob_ds = [nc.dram_tensor("obs", [NT * 128, HW], F32).ap()] * 2
        ada_d = nc.dram_tensor("ada_s", [2, 2, 6 * D], F32).ap()

    wbf = {}
    for l_ in range(2):
        if l_ > 0:
            wbf[("w_in", l_)] = nc.dram_tensor("w_in_bf%d" % l_, [D, PROJ], BF16).ap()
            wbf[("w_out", l_)] = nc.dram_tensor("w_out_bf%d" % l_, [D, D], BF16).ap()
        wbf[("w_up", l_)] = nc.dram_tensor("w_up_bf%d" % l_, [D, 2 * DFF], BF16).ap()
        wbf[("w_dn", l_)] = nc.dram_tensor("w_dn_bf%d" % l_, [DFF, D], BF16).ap()
    wbf_t = {k: Tile("wbf_%s%d" % k) for k in wbf}

    P = Prog(nc)
    ARENA_BYTES = 207 * 1024
    arena_t = es.enter_context(nc.sbuf_tensor("arena", [128, ARENA_BYTES // 2], BF16))
    A = Arena(P, arena_t[:], ARENA_BYTES)
    ps_bufs = []
    for i in range(8):
        pt = es.enter_context(nc.psum_tensor("ps%d" % i, [128, 512], F32))
        ps_bufs.append(Buf(pt[:], Tile("ps%d" % i)))
    PS = Ring(ps_bufs)

    if debug:
        xA_ts = [[Tile("xA%d" % i) for i in range(NT)] for l in range(2)]
        xB_ts = [[Tile("xB%d" % i) for i in range(NT)] for l in range(2)]
        ob_ts = [[Tile("ob%d" % i) for i in range(NT)] for l in range(2)]
    else:
        xA_ts = [[Tile("xA%d" % i) for i in range(NT)]] * 2
        xB_ts = [[Tile("xB%d" % i) for i in range(NT)]] * 2
        ob_ts = [[Tile("ob%d" % i) for i in range(NT)]] * 2
    ada_t = [Tile("ada%d" % l) for l in range(2)]
    out_t = [Tile("out%d" % i) for i in range(NT_LAT)]

    def dma(q, out_ap, in_ap, reads, writes, **kw):
        return P.op(q, lambda e: e.dma_start(out=out_ap, in_=in_ap, **kw), reads, writes, dma=True)

    def mm(out_ap, lhsT, rhs, start, stop, reads, writes, skip=False):
        if skip:
            return P.op("pe", lambda e: e.matmul(out_ap, lhsT, rhs, start=start, stop=stop, skip_group_check=True),
                        reads, writes)
        return P.op("pe", lambda e: e.matmul(out_ap, lhsT, rhs, start=start, stop=stop), reads, writes)

    def act(out_ap, in_ap, func, reads, writes, eng="act", **kw):
        return P.op(eng, lambda e: e.activation(out=out_ap, in_=in_ap, func=func, **kw), reads, writes)

    def tt(eng, out_ap, a, b, op, reads, writes):
        return P.op(eng, lambda e: e.tensor_tensor(out_ap, a, b, op), reads, writes)

    def ts(eng, out_ap, a, s1, s2, op0, op1, reads, writes):
        if op1 is None:
            return P.op(eng, lambda e: e.tensor_scalar(out_ap, a, s1, None, op0), reads, writes)
        return P.op(eng, lambda e: e.tensor_scalar(out_ap, a, s1, s2, op0, op1), reads, writes)

    def stt(eng, out_ap, in0, scalar, in1, op0, op1, reads, writes):
        return P.op(eng, lambda e: e.scalar_tensor_tensor(out=out_ap, in0=in0, scalar=scalar, in1=in1,
                                                          op0=op0, op1=op1), reads, writes)

    def cp(eng, out_ap, in_ap, reads, writes):
        if eng == "act":
            return P.op("act", lambda e: e.copy(out_ap, in_ap), reads, writes)
        return P.op(eng, lambda e: e.tensor_copy(out_ap, in_ap), reads, writes)

    def recip(out_ap, in_ap, reads, writes):
        return P.op("dve", lambda e: e.reciprocal(out_ap, in_ap), reads, writes)

    def rsqrt_cols(col_out, col_in, scale, buf, tmpcol):
        ts("dve", tmpcol, col_in, scale, EPS, ALU.mult, ALU.add, T(buf), T(buf))
        act(tmpcol, tmpcol, AF.Ln, T(buf), T(buf))
        act(col_out, tmpcol, AF.Exp, T(buf), T(buf), scale=-0.5)

    cst = A.alloc("cst", [1024], F32)
    dma("sp", cst[:], consts_d[:, :], [], T(cst))
    ident_f = cst[:, 0:128]
    maskF = cst[:, 128:256]
    maskB = cst[:, 256:384]
    scanmask = cst[:, 384:896]
    rowmask = cst[:, 896:900]
    ident_b = A.alloc("ident_b", [128], BF16)
    cp("dve", ident_b[:], ident_f, T(cst), T(ident_b))

    lbc = A.alloc("lbc", [16], F32)
    scT = A.alloc("scT", [8, 2], F32)

    modc = A.alloc("modc", [2, 4, 8], F32)
    persist_mark = A.mark()

    def prologue():
        m0 = A.mark()
        lbl = A.alloc("lbl", [2, 2, 4], F32)
        for di, src in enumerate((lbf_d, lbb_d)):
            for l in range(2):
                dma("sp", lbl[:, di, l, :], src[l].rearrange("(h k) -> k h", k=128), [], T(lbl),
                    allow_slow_non_contiguous=True)
        for di in range(2):
            o0 = di * 8
            tt("dve", lbc[:, o0:o0 + 4], lbl[:, di, 1, :], lbl[:, di, 0, :], ALU.subtract, T(lbl), T(lbc))
            act(lbc[:, o0:o0 + 4], lbc[:, o0:o0 + 4], AF.Exp, T(lbc), T(lbc), scale=-1.0)
            ts("dve", lbc[:, o0:o0 + 4], lbc[:, o0:o0 + 4], 1.0, None, ALU.add, None, T(lbc), T(lbc))
            recip(lbc[:, o0:o0 + 4], lbc[:, o0:o0 + 4], T(lbc), T(lbc))
            ts("dve", lbc[:, o0 + 4:o0 + 8], lbc[:, o0:o0 + 4], -1.0, 1.0, ALU.mult, ALU.add, T(lbc), T(lbc))
        cct = A.alloc("cct", [8, 2], F32)
        tmp = A.alloc("cctmp", [8, 2], F32)
        dma("sp", cct[:], cc_d[:, :, :], [], T(cct))
        act(tmp[:], cct[:], AF.Exp, T(cct), T(tmp), scale=-1.0)
        ts("dve", tmp[:], tmp[:], 1.0, None, ALU.add, None, T(tmp), T(tmp))
        recip(tmp[:], tmp[:], T(tmp), T(tmp))
        tt("dve", scT[:], cct[:], tmp[:], ALU.mult, T(cct, tmp), T(scT))
        stage = A.ring("adastg", 2, [8, 512], F32)
        adarow = A.alloc("adarow", [6 * D], F32, parts=2)
        brow = A.alloc("brow", [6 * D], F32, parts=2)
        for l in range(n_layers):
            dma("sp", brow[:], b_ada_d[l:l + 1, :].broadcast_to([2, 6 * D]), [], T(brow))
            wv = w_ada_d[l].rearrange("(c p) n -> p c n", p=128)
            for cb in range(12):
                st = stage.get()
                dma("sp", st[:], wv[:, :, cb * 512:(cb + 1) * 512], [], T(st))
                ps = PS.get()
                for c in range(8):
                    mm(ps[0:2, :], scT[:, c, :], st[:, c, :], c == 0, c == 7, T(scT, st), T(ps))
                tt("dve", adarow[:, cb * 512:(cb + 1) * 512], ps[0:2, :], brow[:, cb * 512:(cb + 1) * 512],
                   ALU.add, T(ps, brow), T(adarow))
            dma("sp", ada_d[l], adarow[:], T(adarow), [ada_t[l]])
        A.release(m0)

    def layer_consts(l):
        m0 = A.mark()
        raw = A.alloc("modraw", [2, 4, 8], F32)
        nrm = A.alloc("nrm", [2, 8], F32)
        for m in range(2):
            for k, base in enumerate((0, 1024, 3072, 4096)):
                dma("sp", raw[:, m, k, :], ada_d[l, m, base:base + 1024].rearrange("(c p) -> p c", p=128),
                    [ada_t[l]], T(raw), allow_slow_non_contiguous=True)
        dma("sp", nrm[:, 0, :], norm_mix_d[l].rearrange("(c p) -> p c", p=128), [], T(nrm),
            allow_slow_non_contiguous=True)
        dma("sp", nrm[:, 1, :], norm_ffn_d[l].rearrange("(c p) -> p c", p=128), [], T(nrm),
            allow_slow_non_contiguous=True)
        for m in range(2):
            stt("dve", modc[:, m, 0, :], raw[:, m, 1, :], 1.0, nrm[:, 0, :], ALU.add, ALU.mult, T(raw, nrm), T(modc))
            cp("dve", modc[:, m, 1, :], raw[:, m, 0, :], T(raw), T(modc))
            stt("dve", modc[:, m, 2, :], raw[:, m, 3, :], 1.0, nrm[:, 1, :], ALU.add, ALU.mult, T(raw, nrm), T(modc))
            cp("dve", modc[:, m, 3, :], raw[:, m, 2, :], T(raw), T(modc))
        A.release(m0)

    def src_tile_ap(l, i, stage):
        if stage == "mix":
            if l == 0:
                if i < NT_CTX:
                    return ctx_d[i * 128:(i + 1) * 128, :], []
                return x_d[(i - 2) * 128:(i - 1) * 128, :], []
            return xB_ds[l - 1][i * 128:(i + 1) * 128, :], [xB_ts[l - 1][i]]
        return xA_ds[l][i * 128:(i + 1) * 128, :], [xA_ts[l][i]]

    def norm_transpose(xt, m, which, hT_out_fn, fr):
        st = fr["st"].get()
        xs = fr["xs"].get()
        junk = fr["junk"] if fr.get("junk") is not None else xs
        P.op("pool", lambda e: e.memset(st[:, 0:1], 0.0), [], T(st))
        act(junk[:], xt[:], AF.Square, T(xt), T(junk, st), accum_out=st[:, 0:1])
        rsqrt_cols(st[:, 2:3], st[:, 0:1], 1.0 / D, st, st[:, 1:2])
        act(xs[:], xt[:], AF.Identity, T(xt, st), T(xs), scale=st[:, 2:3])
        gi, si = (0, 1) if which == "mix" else (2, 3)
        for half in range(2):
            ps = fr.get("PS", PS).get()
            for cc in range(4):
                c = half * 4 + cc
                mm(ps[:, cc * 128:(cc + 1) * 128], xs[:, c * 128:(c + 1) * 128], ident_b[:], True, True,
                   T(xs, ident_b), T(ps))
            for cc in range(4):
                c = half * 4 + cc
                eng = "act" if cc % 2 == 0 else "dve"
                if eng == "act":
                    act(hT_out_fn(c), ps[:, cc * 128:(cc + 1) * 128], AF.Identity, T(ps, modc), fr["hT_w"],
                        scale=modc[:, m, gi, c:c + 1], bias=modc[:, m, si, c:c + 1])
                else:
                    ts("dve", hT_out_fn(c), ps[:, cc * 128:(cc + 1) * 128], modc[:, m, gi, c:c + 1],
                       modc[:, m, si, c:c + 1], ALU.mult, ALU.add, T(ps, modc), fr["hT_w"])

    def mixer_phase(l):
        last = (l == n_layers - 1)
        xA_d, xA_t, ob_d, ob_t = xA_ds[l], xA_ts[l], ob_ds[l], ob_ts[l]
        m0 = A.mark()
        w_in = A.alloc("w_in", [8, PROJ], BF16)
        w_out = A.alloc("w_out", [8, D], BF16)
        if l == 0:
            wiv = w_in_d[l].rearrange("(c p) n -> p c n", p=128)
            wov = w_out_d[l].rearrange("(c p) n -> p c n", p=128)
            for c in range(8):
                dma("pool", w_in[:, c, :], wiv[:, c, :], [], T(w_in))
            for c in range(0, 8, 4):
                dma("pool", w_out[:, c:c + 4, :], wov[:, c:c + 4, :], [], T(w_out))
        else:
            wiv = wbf[("w_in", l)].rearrange("(c p) n -> p c n", p=128)
            wov = wbf[("w_out", l)].rearrange("(c p) n -> p c n", p=128)
            for c in range(8):
                dma("sp", w_in[:, c, :], wiv[:, c, :], [wbf_t[("w_in", l)]], T(w_in))
            for c in range(0, 8, 4):
                dma("sp", w_out[:, c:c + 4, :], wov[:, c:c + 4, :], [wbf_t[("w_out", l)]], T(w_out))
        bg_dmas = []
        if l == 0:
            srcs = {"w_in": w_in_d, "w_out": w_out_d, "w_up": w_up_d, "w_dn": w_down_d}
            for key in (("w_up", 0), ("w_dn", 0), ("w_in", 1), ("w_out", 1), ("w_up", 1), ("w_dn", 1)):
                if key[1] >= n_layers:
                    continue
                dst = wbf[key]
                src_full = srcs[key[0]][key[1]]
                rows = dst.shape[0]
                step = 256
                for r0 in range(0, rows, step):
                    bg_dmas.append((dst[r0:r0 + step, :], src_full[r0:r0 + step, :], wbf_t[key]))
        S_f = A.ring("S_f", 2, [512], F32)
        S_b = A.ring("S_b", 2, [512], F32)
        g1_b = [A.alloc("g1b%d" % m, [1024], F32) for m in range(2)]
        for m in range(2):
            dma("sp", g1_b[m][:], ada_d[l, m:m + 1, 2048:3072].broadcast_to([128, 1024]), [ada_t[l]], T(g1_b[m]))
        hgb = A.alloc("hgb", [512], F32)
        lngb = A.alloc("lngb", [512], F32)
        lnbb = A.alloc("lnbb", [512], F32)
        bsb = A.alloc("bsb", [512], F32)
        wsn = A.alloc("wsn", [4, 128], BF16)
        wsT = A.alloc("wsT", [4, 128], BF16)
        dma("sp", hgb[:], hg_norm_d[l:l + 1, :].broadcast_to([128, 512]), [], T(hgb))
        dma("sp", lngb[:], sgu_g_d[l:l + 1, :].broadcast_to([128, 512]), [], T(lngb))
        dma("sp", lnbb[:], sgu_b_d[l:l + 1, :].broadcast_to([128, 512]), [], T(lnbb))
        dma("sp", bsb[:], b_sp_d[l:l + 1, :, :].rearrange("o h p -> o (h p)").broadcast_to([128, 512]), [], T(bsb))
        dma("pool", wsn[:], w_sp_d[l].rearrange("h p q -> p h q"), [], T(wsn))
        ps = PS.get()
        for h in range(4):
            mm(ps[:, h * 128:(h + 1) * 128], wsn[:, h, :], ident_b[:], True, True, T(wsn, ident_b), T(ps))
        cp("act", wsT[:].rearrange("p h q -> p (h q)"), ps[:], T(ps), T(wsT))

        st_r = A.ring("st", 3, [8], F32)
        junk = A.alloc("junk", [1024], BF16)
        xs_r = A.ring("xs", 2, [1024], BF16)
        xring = A.ring("xt", 2, [1024], F32)
        x2ring = A.ring("xt2", 2, [1024], F32)
        hTring = A.ring("hT", 2, [8, 128], BF16)
        tA = A.ring("tA", 6, [512], F32)
        bl_b = A.alloc("bl", [512], F32)
        eb_b = A.alloc("eb", [512], F32)
        KK_r = A.ring("KK", 2, [4, 4, 128], BF16)
        for b in KK_r.bufs:
            P.op("pool", lambda e, b=b: e.memset(b[:], 0.0), [], T(b))
        qdl_r = A.ring("qdl", 2, [512], BF16)
        qdt_r = A.ring("qdt", 3, [512], BF16)
        k2T_r = A.ring("k2T", 2, [512], BF16)
        k2t_r = A.ring("k2t", 2, [512], BF16)
        v_r = A.ring("vbf", 3, [512], BF16)
        scT_r = A.ring("scTb", 2, [512], BF16)
        ecol_r = A.ring("ecol", 3, [48], F32)
        Sbf_r = A.ring("Sbf", 2, [512], BF16)
        obs_r = A.ring("obs", 2, [512], F32)
        gs_r = A.ring("gs", 3, [512], F32)
        gu_r = A.ring("gu", 2, [512], F32)
        vn_r = A.ring("vn", 2, [512], BF16)
        tB = A.ring("tB", 2, [512], F32)
        col_r = A.ring("col", 6, [16], F32)
        ohg_r = A.ring("ohg", 2, [512], BF16)
        oTm_r = A.ring("oTm", 4, [4, 128], BF16)
        oTh_r = A.ring("oTh", 2, [4, 128], BF16)
        xn_r = A.ring("xn", 2, [1024], F32)

        def kk_view(KK, off, n):
            base = KK.ap
            return bass.AP(base.tensor, base.offset + off, [list(base.ap[0]), [128, 4], [544, n], [1, 32]])

        def v4(ap):
            return ap.rearrange("p (h c j) -> p h c j", h=4, c=4)

        def sigmoid_act(dst, src_ap, src_tiles):
            act(dst[:], src_ap, AF.Exp, src_tiles, T(dst), scale=-1.0)
            act(dst[:], dst[:], AF.Ln, T(dst), T(dst), bias=1.0)
            act(dst[:], dst[:], AF.Exp, T(dst), T(dst), scale=-1.0)

        def proj_fm(ps, hT, col0):
            for h in range(4):
                for c in range(8):
                    mm(ps[:, h * 128:(h + 1) * 128], w_in[:, c, col0 + h * 128:col0 + (h + 1) * 128], hT[:, c, :],
                       c == 0, c == 7, T(w_in, hT), T(ps))

        def proj_tm(ps, hT, col0):
            for c in range(8):
                mm(ps[:], hT[:, c, :], w_in[:, c, col0:col0 + 512], c == 0, c == 7, T(w_in, hT), T(ps))

        Sstate = {}

        def tile_gen(i, dirn, Sring):
            is_ctx = i < NT_CTX
            want_o = not (is_ctx and last)
            full = want_o and dirn == "f"
            m = 1 if is_ctx else 0
            di = 0 if dirn == "f" else 1
            edge = 31 if dirn == "f" else 0
            xt = xring.get()
            src, rd = src_tile_ap(l, i, "mix")
            dma("sp", xt[:], src, rd, T(xt))
            st = st_r.get()
            xs = xs_r.get()
            P.op("pool", lambda e: e.memset(st[:, 0:1], 0.0), [], T(st))
            act(junk[:], xt[:], AF.Square, T(xt), T(junk, st), accum_out=st[:, 0:1])
            rsqrt_cols(st[:, 2:3], st[:, 0:1], 1.0 / D, st, st[:, 1:2])
            act(xs[:], xt[:], AF.Identity, T(xt, st), T(xs), scale=st[:, 2:3])
            yield
            hT = hTring.get()
            for half in range(2):
                ps = PS.get()
                for cc in range(4):
                    c = half * 4 + cc
                    mm(ps[:, cc * 128:(cc + 1) * 128], xs[:, c * 128:(c + 1) * 128], ident_b[:], True, True,
                       T(xs, ident_b), T(ps))
                for cc in range(4):
                    c = half * 4 + cc
                    if cc % 2 == 0:
                        act(hT[:, c, :], ps[:, cc * 128:(cc + 1) * 128], AF.Identity, T(ps, modc), T(hT),
                            scale=modc[:, m, 0, c:c + 1], bias=modc[:, m, 1, c:c + 1])
                    else:
                        ts("dve", hT[:, c, :], ps[:, cc * 128:(cc + 1) * 128], modc[:, m, 0, c:c + 1],
                           modc[:, m, 1, c:c + 1], ALU.mult, ALU.add, T(ps, modc), T(hT))
            yield
            psF = PS.get()
            proj_fm(psF, hT, C_FF if dirn == "f" else C_FB)
            psV = PS.get()
            proj_tm(psV, hT, C_I)
            if full:
                psUu = PS.get()
                proj_fm(psUu, hT, C_U)
                psVm = PS.get()
                proj_tm(psVm, hT, C_V)
            if want_o:
                psQ = PS.get()
                proj_fm(psQ, hT, C_Q)
            if full:
                psG = PS.get()
                proj_tm(psG, hT, C_G)
            t1, t2, t3, t4, t5, t6 = tA.get(), tA.get(), tA.get(), tA.get(), tA.get(), tA.get()
            vbf = v_r.get()
            if full:
                gu = gu_r.get()
                act(gu[:], psUu[:], AF.Gelu, T(psUu), T(gu))
                act(t6[:], psVm[:], AF.Gelu, T(psVm), T(t6))
            act(t1[:], psF[:], AF.Exp, T(psF), T(t1), scale=-1.0)
            cp("act", vbf[:], psV[:], T(psV), T(vbf))
            if want_o:
                sigmoid_act(t4, psQ[:], T(psQ))
                tt("dve", t4[:], psQ[:], t4[:], ALU.mult, T(psQ, t4), T(t4))
            if full:
                gs = gs_r.get()
                sigmoid_act(gs, psG[:], T(psG))
                tt("dve", gs[:], psG[:], gs[:], ALU.mult, T(psG, gs), T(gs))
            act(t1[:], t1[:], AF.Ln, T(t1), T(t1), bias=1.0)
            act(t2[:], t1[:], AF.Exp, T(t1), T(t2), scale=-1.0)
            if l > 0:
                t23 = t2[:].rearrange("p (h j) -> p h j", h=4)
                for h in range(4):
                    ts("dve", t23[:, h, :], t23[:, h, :], lbc[:, di * 8 + 4 + h:di * 8 + 5 + h],
                       lbc[:, di * 8 + h:di * 8 + h + 1], ALU.mult, ALU.add, T(t2, lbc), T(t2))
                act(t1[:], t2[:], AF.Ln, T(t2), T(t1))
            act(t2[:], t2[:], AF.Identity, T(t2), T(t2), scale=-1.0, bias=1.0)
            sop = ALU.subtract if l == 0 else ALU.add
            P.op("dve", lambda e: e.tensor_tensor_scan(bl_b[:], scanmask, t1[:], 0.0, ALU.mult, sop),
                 T(cst, t1), T(bl_b))
            Bsrc = bl_b
            if dirn == "b":
                tt("dve", t1[:], bl_b[:], t1[:], ALU.add if l == 0 else ALU.subtract, T(bl_b, t1), T(t1))
                tt("dve", v4(t3[:]), v4(bl_b[:])[:, :, :, 31:32].broadcast_to([128, 4, 4, 32]), v4(t1[:]),
                   ALU.subtract, T(bl_b, t1), T(t3))
                Bsrc = t3
            eb = eb_b
            act(eb[:], Bsrc[:], AF.Exp, T(Bsrc), T(eb))
            act(t1[:], Bsrc[:], AF.Exp, T(Bsrc), T(t1), scale=-1.0)
            eb4 = v4(eb[:])
            tt("dve", t1[:], t2[:], t1[:], ALU.mult, T(t2, t1), T(t1))
            tt("dve", v4(t5[:]), v4(t1[:]), eb4[:, :, :, edge:edge + 1].broadcast_to([128, 4, 4, 32]), ALU.mult,
               T(t1, eb), T(t5))
            ecol = ecol_r.get()
            Plo = ecol[:, 0:16].rearrange("p (h c) -> p h c", c=4)
            Phi = ecol[:, 16:32].rearrange("p (h c) -> p h c", c=4)
            Etile = ecol[:, 32:36]
            e12 = ecol[:, 36:40]
            ec = [eb4[:, :, c, edge] for c in range(4)]
            P.op("pool", lambda e: e.memset(ecol[:, 0:32], 1.0), [], T(ecol))
            cp("pool", Plo[:, :, 1], ec[0], T(eb), T(ecol))
            tt("pool", Plo[:, :, 2], Plo[:, :, 1], ec[1], ALU.mult, T(eb, ecol), T(ecol))
            tt("pool", Plo[:, :, 3], Plo[:, :, 2], ec[2], ALU.mult, T(eb, ecol), T(ecol))
            cp("pool", Phi[:, :, 2], ec[3], T(eb), T(ecol))
            tt("pool", Phi[:, :, 1], Phi[:, :, 2], ec[2], ALU.mult, T(eb, ecol), T(ecol))
            tt("pool", Phi[:, :, 0], Phi[:, :, 1], ec[1], ALU.mult, T(eb, ecol), T(ecol))
            tt("pool", Etile, Plo[:, :, 3], ec[3], ALU.mult, T(eb, ecol), T(ecol))
            tt("pool", e12, ec[1], ec[2], ALU.mult, T(eb), T(ecol))
            Ecum, Esuf = (Plo, Phi) if dirn == "f" else (Phi, Plo)
            k2T = k2T_r.get()
            tt("dve", v4(k2T[:]), v4(t5[:]), Esuf.rearrange("p h (c o) -> p h c o", o=1).broadcast_to([128, 4, 4, 32]),
               ALU.mult, T(t5, ecol), T(k2T))
            if want_o:
                KK = KK_r.get()
                k5 = v4(t5[:])
                e_mid = eb4[:, :, 1:3, edge:edge + 1].broadcast_to([128, 4, 2, 32])
                e12b = e12.rearrange("p (h a b) -> p h a b", a=1, b=1).broadcast_to([128, 4, 1, 32])
                cp("pool", kk_view(KK, 0, 4), v4(t1[:]), T(t1), T(KK))
                if dirn == "f":
                    cp("pool", kk_view(KK, 512, 3), k5[:, :, 0:3, :], T(t5), T(KK))
                    tt("pool", kk_view(KK, 1024, 2), k5[:, :, 0:2, :], e_mid, ALU.mult, T(t5, eb), T(KK))
                    tt("pool", kk_view(KK, 1536, 1), k5[:, :, 0:1, :], e12b, ALU.mult, T(t5, ecol), T(KK))
                else:
                    cp("pool", kk_view(KK, 32, 3), k5[:, :, 1:4, :], T(t5), T(KK))
                    tt("pool", kk_view(KK, 64, 2), k5[:, :, 2:4, :], e_mid, ALU.mult, T(t5, eb), T(KK))
                    tt("pool", kk_view(KK, 96, 1), k5[:, :, 3:4, :], e12b, ALU.mult, T(t5, ecol), T(KK))
                stt("dve", t4[:], t4[:], float(128 ** -0.5), eb[:], ALU.mult, ALU.mult, T(t4, eb), T(t4))
                qdl = qdl_r.get()
                cp("dve", qdl[:], t4[:], T(t4), T(qdl))
                qdt = qdt_r.get()
                tt("dve", v4(qdt[:]), v4(t4[:]), Ecum.rearrange("p h (c o) -> p h c o", o=1).broadcast_to([128, 4, 4, 32]),
                   ALU.mult, T(t4, ecol), T(qdt))
            if full:
                gv3 = t6[:].rearrange("p (h v) -> p h v", h=4)
                col2 = col_r.get()
                P.op("dve", lambda e: e.tensor_reduce(col2[:, 0:4], t6[:].rearrange("p (h v) -> p h v", h=4),
                                                      AX.X, ALU.add), T(t6), T(col2))
                ts("dve", col2[:, 0:4], col2[:, 0:4], 1.0 / 128, None, ALU.mult, None, T(col2), T(col2))
                tt("dve", gv3, gv3, col2[:, 0:4].rearrange("p (h o) -> p h o", o=1).broadcast_to([128, 4, 128]),
                   ALU.subtract, T(t6, col2), T(t6))
                act(t2[:], t6[:], AF.Square, T(t6), T(t2))
                P.op("dve", lambda e: e.tensor_reduce(col2[:, 4:8], t2[:].rearrange("p (h v) -> p h v", h=4),
                                                      AX.X, ALU.add), T(t2), T(col2))
                rsqrt_cols(col2[:, 12:16], col2[:, 4:8], 1.0 / 128, col2, col2[:, 8:12])
                tt("dve", gv3, gv3, col2[:, 12:16].rearrange("p (h o) -> p h o", o=1).broadcast_to([128, 4, 128]),
                   ALU.mult, T(t6, col2), T(t6))
                tt("pool", t6[:], t6[:], lngb[:], ALU.mult, T(t6, lngb), T(t6))
                vn = vn_r.get()
                tt("pool", vn[:], t6[:], lnbb[:], ALU.add, T(t6, lnbb), T(vn))
            yield
            psK = PS.get()
            for h in range(4):
                mm(psK[:, h * 128:(h + 1) * 128], k2T[:, h * 128:(h + 1) * 128], ident_b[:], True, True,
                   T(k2T, ident_b), T(psK))
            k2t = k2t_r.get()
            cp("act", k2t[:], psK[:], T(psK), T(k2t))
            if want_o:
                psS = PS.get()
                for ct in range(4):
                    for h in range(4):
                        mm(psS[:, h * 128 + ct * 32:h * 128 + ct * 32 + 32], KK[:, ct, h, :],
                           qdl[:, h * 128 + ct * 32:h * 128 + ct * 32 + 32], True, True, T(KK, qdl), T(psS))
                scTb = scT_r.get()
                mk = maskF if dirn == "f" else maskB
                tt("dve", scTb[:].rearrange("p (h t) -> p h t", h=4), psS[:].rearrange("p (h t) -> p h t", h=4),
                   mk.rearrange("p (o t) -> p o t", o=1).broadcast_to([128, 4, 128]), ALU.mult, T(psS, cst), T(scTb))
            if full:
                psZ = PS.get()
                for h in range(4):
                    mm(psZ[:, h * 128:(h + 1) * 128], vn[:, h * 128:(h + 1) * 128], wsT[:, h, :], True, True,
                       T(vn, wsT), T(psZ))
                oTm = oTm_r.get()
                tz = tB.get()
                tt("dve", tz[:], psZ[:], bsb[:], ALU.add, T(psZ, bsb), T(tz))
                tt("dve", oTm[:].rearrange("p h t -> p (h t)"), tz[:], gu[:], ALU.mult, T(tz, gu), T(oTm))
                obs = obs_r.get()
                dma("sp", obs[:], ob_d[i * 128:(i + 1) * 128, :], [ob_t[i]], T(obs))
            yield
            S_old = Sring.bufs[Sstate[dirn] % 2]
            S_new = Sring.bufs[(Sstate[dirn] + 1) % 2]
            Sstate[dirn] += 1
            if want_o:
                Sbf = Sstate[dirn + "bf"]
                psO = PS.get()
                for h in range(4):
                    mm(psO[:, h * 128:(h + 1) * 128], scTb[:, h * 128:(h + 1) * 128], vbf[:, h * 128:(h + 1) * 128],
                       h == 0, False, T(scTb, vbf), T(psO), skip=True)
                for h in range(4):
                    mm(psO[:, h * 128:(h + 1) * 128], qdt[:, h * 128:(h + 1) * 128], Sbf[:, h * 128:(h + 1) * 128],
                       False, True, T(qdt, Sbf), T(psO), skip=True)
            psU = PS.get()
            for h in range(4):
                mm(psU[:, h * 128:(h + 1) * 128], k2t[:, h * 128:(h + 1) * 128], vbf[:, h * 128:(h + 1) * 128],
                   True, True, T(k2t, vbf), T(psU))
            for h in range(4):
                stt("dve", S_new[:, h * 128:(h + 1) * 128], S_old[:, h * 128:(h + 1) * 128], Etile[:, h:h + 1],
                    psU[:, h * 128:(h + 1) * 128], ALU.mult, ALU.add, T(S_old, ecol, psU), T(S_new))
            Sbf_n = Sbf_r.get()
            cp("act", Sbf_n[:], S_new[:], T(S_new), T(Sbf_n))
            Sstate[dirn + "bf"] = Sbf_n
            if not want_o:
                return
            if dirn == "b":
                obs = obs_r.get()
                cp("act", obs[:], psO[:], T(psO), T(obs))
                dma("pool", ob_d[i * 128:(i + 1) * 128, :], obs[:], T(obs), [ob_t[i]])
                return
            osum, osq = tB.get(), tB.get()
            col = col_r.get()
            tt("dve", osum[:], psO[:], obs[:], ALU.add, T(psO, obs), T(osum))
            act(osq[:], osum[:], AF.Square, T(osum), T(osq))
            P.op("dve", lambda e: e.tensor_reduce(col[:, 0:4], osq[:].rearrange("p (h v) -> p h v", h=4),
                                                  AX.X, ALU.add), T(osq), T(col))
            rsqrt_cols(col[:, 8:12], col[:, 0:4], 1.0 / 128, col, col[:, 4:8])
            o3 = osum[:].rearrange("p (h v) -> p h v", h=4)
            tt("dve", o3, o3, col[:, 8:12].rearrange("p (h o) -> p h o", o=1).broadcast_to([128, 4, 128]), ALU.mult,
               T(osum, col), T(osum))
            tt("pool", osum[:], osum[:], hgb[:], ALU.mult, T(osum, hgb), T(osum))
            ohg = ohg_r.get()
            tt("dve", ohg[:], osum[:], gs[:], ALU.mult, T(osum, gs), T(ohg))
            yield
            psT2 = PS.get()
            for h in range(4):
                mm(psT2[:, h * 128:(h + 1) * 128], ohg[:, h * 128:(h + 1) * 128], ident_b[:], True, True,
                   T(ohg, ident_b), T(psT2))
            oTh = oTh_r.get()
            cp("act", oTh[:].rearrange("p h t -> p (h t)"), psT2[:], T(psT2), T(oTh))
            xt2 = x2ring.get()
            dma("sp", xt2[:], src, rd, T(xt2))
            yield
            xn = xn_r.get()
            for n in range(2):
                psW = PS.get()
                for c in range(8):
                    lhs = oTh[:, c, :] if c < 4 else oTm[:, c - 4, :]
                    mm(psW[:], lhs, w_out[:, c, n * 512:(n + 1) * 512], c == 0, c == 7, T(oTh, oTm, w_out), T(psW))
                tt("dve", xn[:, n * 512:(n + 1) * 512], psW[:], g1_b[m][:, n * 512:(n + 1) * 512], ALU.mult,
                   T(psW, g1_b[m]), T(xn))
            tt("pool", xn[:], xn[:], xt2[:], ALU.add, T(xn, xt2), T(xn))
            dma("pool", xA_d[i * 128:(i + 1) * 128, :], xn[:], T(xn), [xA_t[i]])

        def run_pass(order, dirn, Sring):
            P.op("pool", lambda e: e.memset(Sring.bufs[0][:], 0.0), [], T(Sring.bufs[0]))
            Sstate[dirn] = 0
            Sbf0 = Sbf_r.get()
            P.op("pool", lambda e: e.memset(Sbf0[:], 0.0), [], T(Sbf0))
            Sstate[dirn + "bf"] = Sbf0
            live = []
            pending = list(order)
            while pending or live:
                for g in list(live):
                    try:
                        next(g)
                    except StopIteration:
                        live.remove(g)
                if pending:
                    g = tile_gen(pending.pop(0), dirn, Sring)
                    next(g)
                    live.append(g)
                if bg_dmas and len(live) >= 3:
                    o_ap, i_ap, wt = bg_dmas.pop(0)
                    dma("pool", o_ap, i_ap, [], [wt])

        run_pass([1, 0] + list(range(NT - 1, NT_CTX - 1, -1)), "b", S_b)
        run_pass(list(range(NT)), "f", S_f)
        while bg_dmas:
            o_ap, i_ap, wt = bg_dmas.pop(0)
            dma("pool", o_ap, i_ap, [], [wt])
        A.release(m0)


    def ffn_phase(l):
        last = (l == n_layers - 1)
        xA_d, xA_t, xB_d, xB_t = xA_ds[l], xA_ts[l], xB_ds[l], xB_ts[l]
        m0 = A.mark()
        w_up = A.alloc("w_up", [8, 2 * DFF], BF16)
        w_dn = A.alloc("w_dn", [NFC, D], BF16)
        wuv = wbf[("w_up", l)].rearrange("(c p) n -> p c n", p=128)
        wdv = wbf[("w_dn", l)].rearrange("(c p) n -> p c n", p=128)
        for c in range(8):
            for hlf in range(2):
                dma("sp", w_up[:, c, hlf * DFF:(hlf + 1) * DFF], wuv[:, c, hlf * DFF:(hlf + 1) * DFF],
                    [wbf_t[("w_up", l)]], T(w_up))
        for c in range(0, NFC, 2):
            dma("sp", w_dn[:, c:c + 2, :], wdv[:, c:c + 2, :], [wbf_t[("w_dn", l)]], T(w_dn))
        cw = A.alloc("cw", [NFC, 9], F32)
        cb = A.alloc("cb", [NFC], F32)
        for t9 in range(9):
            dma("sp", cw[:, :, t9], conv_w_d[l, t9 // 3, t9 % 3, :].rearrange("(c p) -> p c", p=128), [], T(cw),
                allow_slow_non_contiguous=True)
        dma("sp", cb[:], conv_b_d[l].rearrange("(c p) -> p c", p=128), [], T(cb), allow_slow_non_contiguous=True)
        g2_b = [A.alloc("g2b%d" % m, [1024], F32) for m in range(2)]
        for m in range(2):
            dma("sp", g2_b[m][:], ada_d[l, m:m + 1, 5120:6144].broadcast_to([128, 1024]), [ada_t[l]], T(g2_b[m]))
        nfb = None
        if last:
            nfb = A.alloc("nfb", [1024], F32)
            dma("sp", nfb[:], norm_final_d.rearrange("(o n) -> o n", o=1).broadcast_to([128, 1024]), [], T(nfb))

        fr = {
            "st": A.ring("fst", 2, [8], F32),
            "junk": None,
            "xs": A.ring("fxs", 1, [1024], BF16),
        }
        xring = A.ring("fxt", 2, [1024], F32)
        h2T = A.alloc("h2T", [8, 640], BF16)
        mT = A.alloc("mT", [NFC, 512], BF16)
        Gsb_r = A.ring("Gsb", 2, [640], F32)
        acc_r = A.ring("acc", 2, [512], F32)
        xs0 = fr["xs"].bufs[0]
        asb_r = Ring([A.alloc("asb0", [512], F32), Buf(xs0.ap.bitcast(F32), xs0.t)])
        xn_r = A.ring("fxn", 1, [1024], F32)
        fcol_r = A.ring("fcol", 2, [8], F32)

        def block(kind, b):
            if kind == "lat":
                m = 0
                tiles = [2 + 4 * b + k for k in range(4)]
                ntok = 512
                W, nrows, ro = 64, 8, 1
                has_prev, has_next = b > 0, b < 7
                taps = [(ky, kx) for ky in range(3) for kx in range(3)]
            else:
                m = 1
                tiles = [0, 1]
                ntok = 256
                W, nrows, ro = 256, 1, 0
                has_prev = has_next = False
                taps = [(1, kx) for kx in range(3)]
            halo = kind == "lat"
            fr["hT_w"] = T(h2T)
            for k, i in enumerate(tiles):
                xt = xring.get()
                src, rd = src_tile_ap(l, i, "ffn")
                dma("sp", xt[:], src, rd, T(xt))
                norm_transpose(xt, m, "ffn", lambda c, k=k: h2T[:, c, k * 128:(k + 1) * 128], fr)
            if halo:
                xt = xring.get()
                t_first, t_last = tiles[0], tiles[-1]
                ip = t_first - 1 if has_prev else t_first
                inx = t_last + 1 if has_next else t_last
                dma("sp", xt[0:64, :], xA_d[ip * 128 + 64:ip * 128 + 128, :], [xA_t[ip]], T(xt))
                dma("sp", xt[64:128, :], xA_d[inx * 128:inx * 128 + 64, :], [xA_t[inx]], T(xt))
                norm_transpose(xt, m, "ffn", lambda c: h2T[:, c, 512:640], fr)
            pending = None
            for fc in range(NFC):
                psA = PS.get()
                for c in range(8):
                    mm(psA[:, 0:ntok], w_up[:, c, fc * 128:(fc + 1) * 128], h2T[:, c, 0:ntok], c == 0, c == 7,
                       T(w_up, h2T), T(psA))
                asb = asb_r.get()
                cp("act", asb[:, 0:ntok], psA[:, 0:ntok], T(psA), T(asb))
                psG = PS.get()
                for c in range(8):
                    mm(psG[:, 0:ntok], w_up[:, c, DFF + fc * 128:DFF + (fc + 1) * 128], h2T[:, c, 0:ntok], c == 0, c == 7,
                       T(w_up, h2T), T(psG))
                Gsb = Gsb_r.get()
                if halo:
                    psH = PS.get()
                    for c in range(8):
                        mm(psH[:, 0:128], w_up[:, c, DFF + fc * 128:DFF + (fc + 1) * 128], h2T[:, c, 512:640], c == 0,
                           c == 7, T(w_up, h2T), T(psH))
                    G3 = Gsb[:].rearrange("p (r w) -> p r w", w=64)
                    cp("act", Gsb[:, 64:576], psG[:, 0:512], T(psG), T(Gsb))
                    if has_prev:
                        cp("act", Gsb[:, 0:64], psH[:, 0:64], T(psH), T(Gsb))
                    else:
                        P.op("pool", lambda e, Gsb=Gsb: e.memset(Gsb[:, 0:64], 0.0), [], T(Gsb))
                    if has_next:
                        cp("act", Gsb[:, 576:640], psH[:, 64:128], T(psH), T(Gsb))
                    else:
                        P.op("pool", lambda e, Gsb=Gsb: e.memset(Gsb[:, 576:640], 0.0), [], T(Gsb))
                else:
                    G3 = Gsb[:, 0:ntok].rearrange("p (r w) -> p r w", w=W)
                    cp("act", Gsb[:, 0:ntok], psG[:, 0:ntok], T(psG), T(Gsb))
                acc = acc_r.get()
                a3 = acc[:, 0:ntok].rearrange("p (r w) -> p r w", w=W)
                act(acc[:, 0:ntok].rearrange("p (r w) -> p r w", w=W), G3[:, ro:ro + nrows, :], AF.Identity,
                    T(Gsb, cw, cb), T(acc), scale=cw[:, fc, 4:5], bias=cb[:, fc:fc + 1])
                if pending is not None:
                    act(pending[2][:, 0:ntok], pending[2][:, 0:ntok], AF.Gelu, T(pending[2]), T(pending[2]))
                    pf, pA, pacc = pending
                    tt("dve", mT[:, pf, 0:ntok], pA[:, 0:ntok], pacc[:, 0:ntok], ALU.mult, T(pA, pacc), T(mT))
                    pending = None
                for (ky, kx) in taps:
                    if ky == 1 and kx == 1:
                        continue
                    r_src = ro + ky - 1
                    if kx == 0:
                        o_ap, i_ap = a3[:, :, 1:W], G3[:, r_src:r_src + nrows, 0:W - 1]
                    elif kx == 1:
                        o_ap, i_ap = a3[:, :, :], G3[:, r_src:r_src + nrows, :]
                    else:
                        o_ap, i_ap = a3[:, :, 0:W - 1], G3[:, r_src:r_src + nrows, 1:W]
                    t9 = ky * 3 + kx
                    stt("dve", o_ap, i_ap, cw[:, fc, t9:t9 + 1], o_ap, ALU.mult, ALU.add, T(Gsb, cw, acc), T(acc))
                if pending is not None:
                    pf, pA, pacc = pending
                    tt("dve", mT[:, pf, 0:ntok], pA[:, 0:ntok], pacc[:, 0:ntok], ALU.mult, T(pA, pacc), T(mT))
                pending = (fc, asb, acc)
            pf, pA, pacc = pending
            act(pacc[:, 0:ntok], pacc[:, 0:ntok], AF.Gelu, T(pacc), T(pacc))
            tt("dve", mT[:, pf, 0:ntok], pA[:, 0:ntok], pacc[:, 0:ntok], ALU.mult, T(pA, pacc), T(mT))
            for k, i in enumerate(tiles):
                xt = xring.get()
                src, rd = src_tile_ap(l, i, "ffn")
                dma("sp", xt[:], src, rd, T(xt))
                xn = xn_r.get()
                for n in range(2):
                    psW = PS.get()
                    for fc in range(NFC):
                        mm(psW[:], mT[:, fc, k * 128:(k + 1) * 128], w_dn[:, fc, n * 512:(n + 1) * 512], fc == 0,
                           fc == NFC - 1, T(mT, w_dn), T(psW))
                    tt("dve", xn[:, n * 512:(n + 1) * 512], psW[:], g2_b[m][:, n * 512:(n + 1) * 512], ALU.mult,
                       T(psW, g2_b[m]), T(xn))
                tt("dve", xn[:], xn[:], xt[:], ALU.add, T(xn, xt), T(xn))
                if last:
                    col = fcol_r.get()
                    P.op("pool", lambda e, col=col: e.memset(col[:, 0:1], 0.0), [], T(col))
                    jk = fr["xs"].bufs[0]
                    act(jk[:], xn[:], AF.Square, T(xn), T(jk, col), accum_out=col[:, 0:1])
                    rsqrt_cols(col[:, 2:3], col[:, 0:1], 1.0 / D, col, col[:, 1:2])
                    stt("dve", xn[:], xn[:], col[:, 2:3], nfb[:], ALU.mult, ALU.mult, T(xn, col, nfb), T(xn))
                    j = i - 2
                    dma("pool", out_d[j * 128:(j + 1) * 128, :], xn[:], T(xn), [out_t[j]])
                else:
                    dma("pool", xB_d[i * 128:(i + 1) * 128, :], xn[:], T(xn), [xB_t[i]])

        if not last:
            block("ctx", 0)
        for b in range(8):
            block("lat", b)
        A.release(m0)

    prologue()
    for l in range(n_layers):
        layer_consts(l)
        mixer_phase(l)
        ffn_phase(l)
    stats = P.emit()
    es.close()
    return nc, stats, A.peak


_CACHE = {}


def kernel(_debug=False, **inputs):
    key = "nc_dbg" if _debug else "nc"
    if key not in _CACHE:
        _CACHE[key] = build_program(debug=_debug)[0]
    nc = _CACHE[key]
    f = lambda a: np.ascontiguousarray(np.asarray(a, dtype=np.float32))
    x = f(inputs["x"])
    c = f(inputs["c"])
    ctx = f(inputs["ctx"])
    c_ctx = f(inputs["c_ctx"])
    consts = host_consts()
    shared = {k: f(inputs[k]) for k in (
        "w_ada", "b_ada", "norm_mix", "norm_ffn", "w_in", "lb_logits_fwd", "lb_logits_bwd", "hg_norm",
        "sgu_norm_g", "sgu_norm_b", "w_spatial", "b_spatial", "w_out", "w_up", "conv_w", "conv_b", "w_down",
        "norm_final")}
    in_maps = []
    for b in range(8):
        cc = np.stack([c[b], c_ctx], axis=-1).reshape(8, 128, 2).transpose(1, 0, 2)
        d = {"x": x[b], "ctx": ctx[b], "cc": np.ascontiguousarray(cc), "consts": consts}
        d.update(shared)
        in_maps.append(d)
    res = run_bass_kernel_spmd(nc, in_maps, core_ids=list(range(8)))
    if _debug:
        return res.results
    return np.stack([np.asarray(res.results[b]["out"], dtype=np.float32) for b in range(8)], axis=0)


if __name__ == "__main__":
    import time
    t0 = time.time()
    nc, stats, peak = build_program()
    print("build", time.time() - t0, stats, "arena peak", peak)
```
